# Optimizing a Trainium2 kernel written in Bass

```python
import jax, jax.numpy as jnp
from jax import lax
import numpy as np

D_MODEL = 1024
BATCH = 4
SEQ = 4096
DEPTH = 1
DEC_BATCH = 128
DEC_SEQ = 4
PAST_LEN = 8192
PAGE_SIZE = 128

EPS = 1e-6
GLA_HEADS = 4
GLA_DK = D_MODEL // 2 // GLA_HEADS
GLA_DV = D_MODEL // GLA_HEADS
GLA_RANK = 16
GLA_TAU = 16.0
GLA_CHUNK = 64
GLA_SCALE = GLA_DK ** -0.5
DSW_GROUPS = ((128, 1), (512, 4), (2048, 16))
DSW_N_GROUPS = len(DSW_GROUPS)
DSW_HEADS = 4
DSW_HEAD_DIM = 64
DSW_SCALE = DSW_HEAD_DIM ** -0.5
BAND_BLOCK = 128
ROPE_THETA = 10000.0
D_FF = -(-8 * D_MODEL // (3 * 256)) * 256
GLA_QK_W = GLA_HEADS * GLA_DK
GLA_V_W = GLA_HEADS * GLA_DV
DSW_W = DSW_N_GROUPS * DSW_HEADS * DSW_HEAD_DIM
DSW_OUT_W = DSW_HEADS * DSW_HEAD_DIM
IN_SIZES = (GLA_QK_W, GLA_QK_W, GLA_V_W, GLA_V_W, GLA_RANK, DSW_W, DSW_W, DSW_W, D_MODEL, D_MODEL)
IN_W = sum(IN_SIZES)
IN_OFFSETS = tuple(int(o) for o in np.cumsum(IN_SIZES)[:-1])

kernel_name = "gla_dilated_window_hybrid_adaln_step"


def _rmsnorm(x, g):
    xf = x.astype(jnp.float32)
    y = xf * lax.rsqrt(jnp.mean(xf * xf, axis=-1, keepdims=True) + EPS)
    return (y * g.astype(jnp.float32)).astype(x.dtype)


def _rope(x, pos):
    half = x.shape[-1] // 2
    inv = ROPE_THETA ** (-jnp.arange(half, dtype=jnp.float32) / half)
    ang = pos[:, None] * inv[None, :]
    cos, sin = jnp.cos(ang)[:, None, :], jnp.sin(ang)[:, None, :]
    xf = x.astype(jnp.float32)
    x1, x2 = xf[..., :half], xf[..., half:]
    return jnp.concatenate([x1 * cos - x2 * sin, x2 * cos + x1 * sin], axis=-1).astype(x.dtype)


def _masked_softmax(s, valid):
    s = jnp.where(valid, s, -jnp.inf)
    m = jnp.max(s, axis=-1, keepdims=True)
    e = jnp.exp(s - m)
    den = jnp.sum(e, axis=-1, keepdims=True)
    return e / den, (m + jnp.log(den))[..., 0]


def _gla(q, k, v, log_a, S0):
    B, T, H, dk = q.shape
    dv = v.shape[-1]
    C = min(GLA_CHUNK, T)
    n = T // C
    f = lambda a: a.astype(jnp.float32).reshape(B, n, C, H, a.shape[-1]).transpose(0, 1, 3, 2, 4)
    qc, kc, vc, lc = f(q) * GLA_SCALE, f(k), f(v), f(log_a)
    b = jnp.cumsum(lc, axis=3)
    b_last = b[:, :, :, -1:, :]
    qg = qc * jnp.exp(b)
    kd = kc * jnp.exp(-b)
    kl = kc * jnp.exp(b_last - b)
    causal = jnp.tril(jnp.ones((C, C), jnp.float32))
    att = jnp.einsum('bnhcd,bnhsd->bnhcs', qg, kd) * causal
    intra = jnp.einsum('bnhcs,bnhsv->bnhcv', att, vc)
    dec = jnp.exp(b_last[:, :, :, 0, :])

    def step(S, inp):
        qg_c, kl_c, v_c, dec_c = inp
        inter = jnp.einsum('bhcd,bhdv->bhcv', qg_c, S)
        S = dec_c[..., None] * S + jnp.einsum('bhcd,bhcv->bhdv', kl_c, v_c)
        return S, inter

    xs = (jnp.moveaxis(qg, 1, 0), jnp.moveaxis(kl, 1, 0), jnp.moveaxis(vc, 1, 0), jnp.moveaxis(dec, 1, 0))
    S_fin, inter = lax.scan(step, S0.astype(jnp.float32), xs)
    o = intra + jnp.moveaxis(inter, 0, 1)
    o = o.transpose(0, 1, 3, 2, 4).reshape(B, T, H, dv)
    return o, S_fin


def _band_attn(q, k, v, max_dist):
    N, L, H, hd = q.shape
    blk = BAND_BLOCK
    nb = -(-L // blk)
    pad = nb * blk - L
    blocks = lambda a: jnp.pad(a, ((0, 0), (0, pad), (0, 0), (0, 0))).reshape(N, nb, blk, H, hd)
    qb, kb, vb = blocks(q), blocks(k), blocks(v)

    def with_prev(a):
        prev = jnp.concatenate([jnp.zeros_like(a[:, :1]), a[:, :-1]], axis=1)
        return jnp.concatenate([prev, a], axis=2)

    kw, vw = with_prev(kb), with_prev(vb)
    s = jnp.einsum('nbqhd,nbkhd->nbhqk', qb, kw).astype(jnp.float32) * DSW_SCALE
    qi = jnp.arange(blk)[:, None] + blk
    ki = jnp.arange(2 * blk)[None, :]
    dist = qi - ki
    kabs = jnp.arange(nb)[:, None, None] * blk - blk + ki[None]
    valid = (dist >= 0) & (dist <= max_dist) & (kabs >= 0)
    p, lse = _masked_softmax(s, valid[None, :, None])
    o = jnp.einsum('nbhqk,nbkhd->nbqhd', p.astype(v.dtype), vw).reshape(N, nb * blk, H, hd)[:, :L]
    lse = lse.transpose(0, 1, 3, 2).reshape(N, nb * blk, H)[:, :L]
    return o, lse


def _dilated_band(q, k, v, dil, n_keys):
    B, T, H, hd = q.shape
    L = T // dil
    fold = lambda a: a.reshape(B, L, dil, H, hd).transpose(0, 2, 1, 3, 4).reshape(B * dil, L, H, hd)
    o, lse = _band_attn(fold(q), fold(k), fold(v), n_keys)
    o = o.reshape(B, dil, L, H, hd).transpose(0, 2, 1, 3, 4).reshape(B, T, H, hd)
    lse = lse.reshape(B, dil, L, H).transpose(0, 2, 1, 3).reshape(B, T, H)
    return o, lse


def _combine(outs, lses):
    w = jax.nn.softmax(jnp.stack(lses, axis=0), axis=0)[..., None]
    return jnp.sum(w * jnp.stack(outs, axis=0).astype(jnp.float32), axis=0)


def _dsw_prompt(q, k, v):
    T = q.shape[1]
    outs, lses, new = [], [], []
    for g, (win, dil) in enumerate(DSW_GROUPS):
        o, lse = _dilated_band(q[:, :, g], k[:, :, g], v[:, :, g], dil, win // dil)
        outs.append(o)
        lses.append(lse)
        keep = min(win, T)
        new.append(jnp.stack([k[:, T - keep:, g], v[:, T - keep:, g]], axis=2))
    return _combine(outs, lses), tuple(new)


def _dsw_sample(q, k, v, caches):
    S = q.shape[1]
    outs, lses, new = [], [], []
    for g, (win, dil) in enumerate(DSW_GROUPS):
        cache = caches[g]
        Wb = cache.shape[1]
        kc = jnp.concatenate([cache[:, :, 0].astype(k.dtype), k[:, :, g]], axis=1)
        vc = jnp.concatenate([cache[:, :, 1].astype(v.dtype), v[:, :, g]], axis=1)
        n_keys = win // dil
        idx = Wb + jnp.arange(S)[:, None] - jnp.arange(n_keys + 1)[None, :] * dil
        valid = idx >= 0
        idxc = jnp.maximum(idx, 0)
        kg, vg = kc[:, idxc], vc[:, idxc]
        s = jnp.einsum('bshd,bskhd->bhsk', q[:, :, g], kg).astype(jnp.float32) * DSW_SCALE
        p, lse = _masked_softmax(s, valid[None, None])
        outs.append(jnp.einsum('bhsk,bskhd->bshd', p.astype(vg.dtype), vg))
        lses.append(lse.transpose(0, 2, 1))
        keep = min(win, Wb + S)
        new.append(jnp.stack([kc[:, Wb + S - keep:], vc[:, Wb + S - keep:]], axis=2))
    return _combine(outs, lses), tuple(new)


def _layer(x, c, pos, S0, caches, norm1_g, norm2_g, w_ada, b_ada, w_in, b_in, w_alpha2, b_alpha2,
           gla_norm_g, w_proj_a, w_proj_b, w_out, w_up, w_down):
    B, T, D = x.shape
    mod = (jax.nn.silu(c) @ w_ada + b_ada).reshape(B, 6, 1, D)
    sh1, sc1, g1, sh2, sc2, g2 = (mod[:, i] for i in range(6))
    h = _rmsnorm(x, norm1_g) * (1 + sc1) + sh1
    proj = h @ w_in + b_in
    gq, gk, gv, gr, glr, dq, dk, dv, ga, gb = jnp.split(proj, IN_OFFSETS, axis=-1)
    heads = lambda a, n: a.reshape(B, T, GLA_HEADS, n)
    log_a = jax.nn.log_sigmoid((glr @ w_alpha2 + b_alpha2).astype(jnp.float32)) / GLA_TAU
    o_a, S_new = _gla(heads(gq, GLA_DK), heads(gk, GLA_DK), heads(gv, GLA_DV), heads(log_a, GLA_DK), S0)
    o_a = (_rmsnorm(o_a, gla_norm_g).reshape(B, T, GLA_V_W) * jax.nn.silu(gr.astype(jnp.float32))).astype(x.dtype)
    dsw = lambda a: a.reshape(B, T, DSW_N_GROUPS * DSW_HEADS, DSW_HEAD_DIM)
    q = _rope(dsw(dq), pos).reshape(B, T, DSW_N_GROUPS, DSW_HEADS, DSW_HEAD_DIM)
    k = _rope(dsw(dk), pos).reshape(B, T, DSW_N_GROUPS, DSW_HEADS, DSW_HEAD_DIM)
    v = dv.reshape(B, T, DSW_N_GROUPS, DSW_HEADS, DSW_HEAD_DIM)
    if caches is None:
        o_b, new_kv = _dsw_prompt(q, k, v)
    else:
        o_b, new_kv = _dsw_sample(q, k, v, caches)
    o_b = o_b.reshape(B, T, DSW_OUT_W).astype(x.dtype)
    merged = jax.nn.sigmoid(ga) * (o_a @ w_proj_a) + jax.nn.sigmoid(gb) * (o_b @ w_proj_b)
    x = x + g1 * (merged @ w_out)
    h2 = _rmsnorm(x, norm2_g) * (1 + sc2) + sh2
    u1, u2 = jnp.split(h2 @ w_up, 2, axis=-1)
    x = x + g2 * ((jax.nn.silu(u1) * u2) @ w_down)
    return x, S_new, new_kv


def setup_inputs(seed: int = 0) -> dict:
    key = jax.random.key(seed)
    ks = jax.random.split(key, 24)
    nrm = lambda k, shape, scale: jax.random.normal(k, shape, jnp.float32) * scale
    D = D_MODEL
    kv_shape = lambda win: (DEPTH, DEC_BATCH, min(win, PAST_LEN), 2, DSW_HEADS, DSW_HEAD_DIM)
    return {
        "x_prompt": nrm(ks[0], (BATCH, SEQ, D), 1.0),
        "x_sample": nrm(ks[1], (DEC_BATCH, DEC_SEQ, D), 1.0),
        "state_gla": nrm(ks[2], (DEPTH, DEC_BATCH, GLA_HEADS, GLA_DK, GLA_DV), 1.0),
        "cache_kv_w128": nrm(ks[3], kv_shape(DSW_GROUPS[0][0]), 1.0),
        "cache_kv_w512": nrm(ks[4], kv_shape(DSW_GROUPS[1][0]), 1.0),
        "cache_kv_w2048": nrm(ks[5], kv_shape(DSW_GROUPS[2][0]), 1.0),
        "c_prompt": nrm(ks[6], (BATCH, D), 1.0),
        "c_sample": nrm(ks[7], (DEC_BATCH, D), 1.0),
        "norm1_g": 1.0 + nrm(ks[8], (DEPTH, D), 0.05),
        "norm2_g": 1.0 + nrm(ks[9], (DEPTH, D), 0.05),
        "w_ada": nrm(ks[10], (DEPTH, D, 6 * D), 0.5 * D ** -0.5),
        "b_ada": nrm(ks[11], (DEPTH, 6 * D), 0.02),
        "w_in": nrm(ks[12], (DEPTH, D, IN_W), D ** -0.5),
        "b_in": nrm(ks[13], (DEPTH, IN_W), 0.02),
        "w_alpha2": nrm(ks[14], (DEPTH, GLA_RANK, GLA_QK_W), GLA_RANK ** -0.5),
        "b_alpha2": nrm(ks[15], (DEPTH, GLA_QK_W), 0.1),
        "gla_norm_g": 1.0 + nrm(ks[16], (DEPTH, GLA_DV), 0.05),
        "w_proj_a": nrm(ks[17], (DEPTH, GLA_V_W, D), GLA_V_W ** -0.5),
        "w_proj_b": nrm(ks[18], (DEPTH, DSW_OUT_W, D), DSW_OUT_W ** -0.5),
        "w_out": nrm(ks[19], (DEPTH, D, D), D ** -0.5),
        "w_up": nrm(ks[20], (DEPTH, D, 2 * D_FF), D ** -0.5),
        "w_down": nrm(ks[21], (DEPTH, D_FF, D), D_FF ** -0.5),
        "normf_g": 1.0 + nrm(ks[22], (D,), 0.05),
    }


def reference(x_prompt, x_sample, state_gla, cache_kv_w128, cache_kv_w512, cache_kv_w2048, c_prompt, c_sample,
              norm1_g, norm2_g, w_ada, b_ada, w_in, b_in, w_alpha2, b_alpha2, gla_norm_g, w_proj_a, w_proj_b,
              w_out, w_up, w_down, normf_g):
    pos_p = jnp.arange(x_prompt.shape[1], dtype=jnp.float32)
    pos_s = PAST_LEN + jnp.arange(x_sample.shape[1], dtype=jnp.float32)
    S0_p = jnp.zeros((x_prompt.shape[0], GLA_HEADS, GLA_DK, GLA_DV), jnp.float32)
    xp, xs = x_prompt, x_sample
    sp_list, kvp_list, ss_list, kvs_list = [], [], [], []
    for l in range(DEPTH):
        lw = (norm1_g[l], norm2_g[l], w_ada[l], b_ada[l], w_in[l], b_in[l], w_alpha2[l], b_alpha2[l],
              gla_norm_g[l], w_proj_a[l], w_proj_b[l], w_out[l], w_up[l], w_down[l])
        xp, s_p, kv_p = _layer(xp, c_prompt, pos_p, S0_p, None, *lw)
        xs, s_s, kv_s = _layer(xs, c_sample, pos_s, state_gla[l],
                               (cache_kv_w128[l], cache_kv_w512[l], cache_kv_w2048[l]), *lw)
        sp_list.append(s_p)
        kvp_list.append(kv_p)
        ss_list.append(s_s)
        kvs_list.append(kv_s)
    y_prompt = _rmsnorm(xp, normf_g)
    y_sample = _rmsnorm(xs, normf_g)
    state_gla_p = jnp.stack(sp_list).astype(x_prompt.dtype)
    kv128_p = jnp.stack([kv[0] for kv in kvp_list])
    kv512_p = jnp.stack([kv[1] for kv in kvp_list])
    kv2048_p = jnp.stack([kv[2] for kv in kvp_list])
    state_gla_s = jnp.stack(ss_list).astype(state_gla.dtype)
    kv128_s = jnp.stack([kv[0] for kv in kvs_list])
    kv512_s = jnp.stack([kv[1] for kv in kvs_list])
    kv2048_s = jnp.stack([kv[2] for kv in kvs_list])
    return (y_prompt, y_sample, state_gla_p, kv128_p, kv512_p, kv2048_p, state_gla_s, kv128_s, kv512_s, kv2048_s)
```

```python
import numpy as np
from contextlib import ExitStack
import concourse.bass as bass
import concourse.mybir as mybir
from concourse.bass_utils import run_bass_kernel_spmd

F32 = mybir.dt.float32
BF16 = mybir.dt.bfloat16
AF = mybir.ActivationFunctionType
ALU = mybir.AluOpType
AX = mybir.AxisListType

ENGS = ("pe", "act", "dve", "pool", "sp")
import os as _os
REORDER_ENGS = tuple(_os.environ.get("REORDER_ENGS", "pe,dve,pool,sp").split(","))
D = 1024
NCORES = 8
TH = 2048
NBLK = 16
SB_ = 16
NS = 64
EPS = 1e-6
IN_W = 7440
O_GQ, O_GK, O_GV, O_GR, O_GLR, O_DQ, O_DK, O_DV, O_GA, O_GB = 0, 512, 1024, 2048, 3072, 3088, 3856, 4624, 5392, 6416
DFF = 2816
SB_GRAN = 256
PS_GRAN = 2048
DR_GRAN = 1 << 14


def _esize(dt):
    return mybir.dt.size(dt)


class Prog:
    N_DMA_SEMS = 40
    N_SW_SEMS = 16

    def __init__(self, nc):
        self.nc = nc
        self.ops = []
        self.tracked_dram = set()
        self.cur_phase = "init"

    def _gran(self, ap):
        t = ap.tensor
        es = _esize(ap.dtype)
        sp = str(t.space)
        dims = list(ap.ap)
        if "DRAM" in sp:
            if t.name not in self.tracked_dram:
                return ()
            lo = ap.offset
            ext = sum((c - 1) * abs(s) for s, c in dims) + 1
            g0, g1 = (lo * es) // DR_GRAN, ((lo + ext) * es - 1) // DR_GRAN
            return [(t.name, g) for g in range(g0, g1 + 1)]
        rowlen = t.shape[1]
        for d in t.shape[2:]:
            rowlen *= d
        col0 = ap.offset % rowlen
        ext = sum((c - 1) * abs(s) for s, c in dims[1:]) + 1
        gr = PS_GRAN if "PSUM" in sp else SB_GRAN
        g0, g1 = (col0 * es) // gr, ((col0 + ext) * es - 1) // gr
        return [(t.name, g) for g in range(g0, g1 + 1)]

    def op(self, eng, fn, r=(), w=(), dma=False):
        rk, wk = set(), set()
        for a in r:
            if a is not None and not isinstance(a, (int, float)):
                rk.update(self._gran(a))
        for a in w:
            wk.update(self._gran(a))
        n_el, nbytes, f32 = 1, 0, 1
        if len(w) > 0:
            n_el = 1
            for d_ in w[0].shape[1:]:
                n_el *= int(d_)
            nbytes = n_el * int(w[0].shape[0]) * _esize(w[0].dtype)
        if eng == "pe" and len(r) > 1:
            n_el = 1
            for d_ in r[1].shape[1:]:
                n_el *= int(d_)
            f32 = 4 if r[1].dtype == F32 else 1
        self.ops.append(dict(eng=eng, fn=fn, r=rk, w=wk, dma=dma, n=n_el, bytes=nbytes, f32=f32, ph=self.cur_phase))

    def mm(self, out, lhsT, rhs, start=True, stop=True, **kw):
        self.op("pe", lambda e: e.matmul(out, lhsT, rhs, start=start, stop=stop, **kw),
                r=[lhsT, rhs] + ([] if start else [out]), w=[out])

    def tr(self, out, in_, ident):
        self.op("pe", lambda e: e.transpose(out, in_, ident), r=[in_, ident], w=[out])

    def act(self, out, in_, func, bias=None, scale=None, accum_out=None, eng="act"):
        kw = {}
        if bias is not None:
            kw["bias"] = bias
        if scale is not None:
            kw["scale"] = scale
        if accum_out is not None:
            kw["accum_out"] = accum_out
        self.op(eng, lambda e: e.activation(out, in_, func, **kw), r=[in_, bias, scale],
                w=[out] + ([accum_out] if accum_out is not None else []))

    def tt(self, out, in0, in1, op, eng="dve"):
        self.op(eng, lambda e: e.tensor_tensor(out, in0, in1, op), r=[in0, in1], w=[out])

    def ts(self, out, in0, s1, op0, s2=None, op1=None, eng="dve", accum_out=None):
        kw = {}
        if op1 is not None:
            kw["op1"] = op1
        if accum_out is not None:
            kw["accum_out"] = accum_out
        self.op(eng, lambda e: e.tensor_scalar(out, in0, s1, s2, op0, **kw), r=[in0, s1, s2],
                w=[out] + ([accum_out] if accum_out is not None else []))

    def stt(self, out, in0, scalar, in1, op0, op1):
        self.op("dve", lambda e: e.scalar_tensor_tensor(out, in0, scalar, in1, op0, op1),
                r=[in0, scalar, in1], w=[out])

    def copy(self, out, in_, eng="dve"):
        if eng == "act":
            self.op("act", lambda e: e.copy(out, in_), r=[in_], w=[out])
        else:
            self.op(eng, lambda e: e.tensor_copy(out, in_), r=[in_], w=[out])

    def memset(self, ap, val, eng="pool"):
        self.op(eng, lambda e: e.memset(ap, val), w=[ap])

    def recip(self, out, in_):
        self.op("dve", lambda e: e.reciprocal(out, in_), r=[in_], w=[out])

    def dma(self, q, out, in_, **kw):
        self.op(q, lambda e: e.dma_start(out, in_, **kw), r=[in_], w=[out], dma=True)

    def _cost(self, o):
        n, e = o["n"], o["eng"]
        if o["dma"]:
            return 60.0
        if e == "pe":
            return 60.0 + max(56.0, 0.42 * n * o.get("f32", 1))
        if e == "act":
            return 220.0 + 0.75 * n
        if e == "dve":
            return 130.0 + 1.05 * n
        return 320.0 + 1.3 * n

    def finalize(self, window=int(_os.environ.get("RWINDOW", "128")), reorder=True, reorder_engs=REORDER_ENGS):
        nc, ops = self.nc, self.ops
        n = len(ops)
        last_w, readers = {}, {}
        deps = [set() for _ in range(n)]
        for j, o in enumerate(ops):
            dj = deps[j]
            for k in o["r"]:
                i = last_w.get(k)
                if i is not None:
                    dj.add(i)
            for k in o["w"]:
                i = last_w.get(k)
                if i is not None:
                    dj.add(i)
                rs = readers.get(k)
                if rs:
                    dj.update(rs)
            for k in o["r"]:
                readers.setdefault(k, []).append(j)
            for k in o["w"]:
                last_w[k] = j
                readers[k] = []
            dj.discard(j)
        self.gaps = {}
        pend = {e: [j for j, o in enumerate(ops) if o["eng"] == e] for e in ENGS}
        head = {e: 0 for e in ENGS}
        free_t = {e: 0.0 for e in ENGS}
        fin = [None] * n
        start_t = [0.0] * n
        order = {e: [] for e in ENGS}
        done_cnt = 0
        taken = [False] * n
        while done_cnt < n:
            best = None
            for e in ENGS:
                lst = pend[e]
                h = head[e]
                while h < len(lst) and taken[lst[h]]:
                    h += 1
                head[e] = h
                cnt = 0
                k = h
                while k < len(lst) and cnt < (window if (reorder and e in reorder_engs) else 1):
                    j = lst[k]
                    k += 1
                    if taken[j]:
                        continue
                    cnt += 1
                    rdy = 0.0
                    ok = True
                    who = -1
                    for i in deps[j]:
                        f = fin[i]
                        if f is None:
                            ok = False
                            break
                        if f > rdy:
                            rdy = f
                            who = i
                    if not ok:
                        continue
                    st = max(rdy, free_t[e])
                    if best is None or st < best[0] or (st == best[0] and j < best[1]):
                        best = (st, j, e, who if rdy > free_t[e] else -1, max(0.0, rdy - free_t[e]))
            assert best is not None, "scheduler deadlock"
            st, j, e, who_, gap_ = best
            o = ops[j]
            if who_ >= 0:
                key_ = (o["ph"], e, ops[who_]["eng"] + ("/dma" if ops[who_]["dma"] else ""))
                self.gaps[key_] = self.gaps.get(key_, 0.0) + gap_
            c = self._cost(o)
            start_t[j] = st
            free_t[e] = st + c
            fin[j] = st + c + ((2000.0 + o["bytes"] / 150.0) if o["dma"] else 60.0)
            taken[j] = True
            order[e].append(j)
            done_cnt += 1
        self.sim_time_us = max(f for f in fin) / 1000.0
        rep = {}
        for j, o in enumerate(ops):
            d = rep.setdefault(o["ph"], dict(t0=1e18, t1=0.0, busy={e: 0.0 for e in ENGS}))
            d["t0"] = min(d["t0"], start_t[j])
            d["t1"] = max(d["t1"], fin[j])
            d["busy"][o["eng"]] += self._cost(o)
        self.phase_report = {k: (round(v["t0"] / 1000), round(v["t1"] / 1000), {e: round(b / 1000) for e, b in v["busy"].items()}) for k, v in rep.items()}
        pos = {}
        for e in ENGS:
            for p_, j in enumerate(order[e]):
                pos[j] = p_
        pools = {"sp": (0, 32), "act": (32, 8), "pool": (40, 16)}
        nsem = 56
        dma_sem_of, dma_val_of, sem_last, sem_cnt = {}, {}, {}, {}
        for e in ENGS:
            rr = 0
            for j in order[e]:
                if ops[j]["dma"]:
                    base, cnt_ = pools[e]
                    s_ = base + rr % cnt_
                    rr += 1
                    if s_ in sem_last:
                        deps[j].add(sem_last[s_])
                    sem_last[s_] = j
                    sem_cnt[s_] = sem_cnt.get(s_, 0) + 1
                    dma_sem_of[j] = s_
                    dma_val_of[j] = 16 * sem_cnt[s_]
        need_inc = [False] * n
        final_deps = [[] for _ in range(n)]
        for j, o in enumerate(ops):
            best = {}
            for i in deps[j]:
                oi = ops[i]
                if not oi["dma"] and not o["dma"] and oi["eng"] == o["eng"] and o["eng"] == "pe":
                    continue
                if oi["dma"]:
                    final_deps[j].append(i)
                    need_inc[i] = True
                else:
                    b_ = best.get(oi["eng"])
                    if b_ is None or pos[b_] < pos[i]:
                        best[oi["eng"]] = i
            for i in best.values():
                final_deps[j].append(i)
                need_inc[i] = True
        val_of = {}
        cnt = {e: 0 for e in ENGS}
        for e in ENGS:
            for j in order[e]:
                if not ops[j]["dma"] and need_inc[j]:
                    cnt[e] += 1
                    val_of[j] = cnt[e]
        self.stats = dict(n_ops=n, cnt=dict(cnt), n_dma=len(dma_sem_of), sim_us=round(self.sim_time_us))
        es = ExitStack()
        esem = {e: es.enter_context(nc.semaphore(f"c_{e}")) for e in ENGS}
        dsem = [es.enter_context(nc.semaphore(f"d_{i}")) for i in range(nsem)]

        def emit_engine(ename, eng):
            waited = {}
            for j in order[ename]:
                o = ops[j]
                for i in sorted(final_deps[j]):
                    oi = ops[i]
                    if oi["dma"]:
                        key, sem, v = ("d", dma_sem_of[i]), dsem[dma_sem_of[i]], dma_val_of[i]
                    else:
                        key, sem, v = ("c", oi["eng"]), esem[oi["eng"]], val_of[i]
                    if waited.get(key, 0) >= v:
                        continue
                    waited[key] = v
                    eng.wait_ge(sem, v)
                ins = o["fn"](eng)
                if o["dma"]:
                    ins.then_inc(dsem[dma_sem_of[j]], 16)
                elif need_inc[j]:
                    ins.then_inc(esem[ename], 1)
            if ename == "sp":
                for s_, c in sem_cnt.items():
                    if waited.get(("d", s_), 0) < 16 * c:
                        eng.wait_ge(dsem[s_], 16 * c)
                for e2 in ("pe", "act", "dve", "pool"):
                    if cnt[e2] > 0:
                        eng.wait_ge(esem[e2], cnt[e2])

        with nc.Block() as block:
            @block.tensor
            def _(e):
                emit_engine("pe", e)

            @block.scalar
            def _(e):
                emit_engine("act", e)

            @block.vector
            def _(e):
                emit_engine("dve", e)

            @block.gpsimd
            def _(e):
                emit_engine("pool", e)

            @block.sync
            def _(e):
                emit_engine("sp", e)
        es.close()


class SBAlloc:
    def __init__(self, nc, es, kib=200):
        self.nbytes = kib * 1024
        self.t32 = es.enter_context(nc.sbuf_tensor("SB", [128, self.nbytes // 4], F32))
        self.t16 = self.t32.bitcast(BF16)
        self.top = 0
        self.hi = self.nbytes
        self.peak = 0

    def mark(self):
        return self.top

    def release(self, m):
        self.top = m

    def alloc(self, shape, dt, hi=False):
        es = _esize(dt)
        nel = int(np.prod(shape))
        nb = (nel * es + 63) // 64 * 64
        if hi:
            self.hi -= nb
            off = self.hi
        else:
            off = self.top
            self.top += nb
        self.peak = max(self.peak, self.top + self.nbytes - self.hi)
        assert self.top <= self.hi, f"SBUF overflow {self.top} {self.hi}"
        base = self.t32 if dt == F32 else self.t16
        ap = base[:, off // es: off // es + nel]
        if len(shape) > 1:
            names = " ".join(f"d{i}" for i in range(len(shape)))
            ap = ap.rearrange(f"p ({names}) -> p {names}", **{f"d{i}": int(s) for i, s in enumerate(shape)})
        return ap


def bcast(ap, shape_pattern):
    pstride = ap.ap[0][0]
    return bass.AP(ap.tensor, ap.offset, [[pstride, ap.ap[0][1]]] + [list(x) for x in shape_pattern])


GLA_SCALE = 128 ** -0.5
DSW_SCALE = 0.125
DSW_DIL = (1, 4, 16)
C_ID, C_SELP, C_SELS, C_ONES, C_TRIL, C_TRIU, C_MD, C_MP, C_T4L, C_T4U = 0, 128, 256, 320, 384, 512, 640, 768, 896, 960


def build(dev=None):
    nc = bass.Bass("TRN2", target_bir_lowering=False)
    es = ExitStack()
    P = Prog(nc)
    di = lambda name, shape: nc.dram_tensor(name, list(shape), F32, kind="ExternalInput").ap()
    do = lambda name, shape: nc.dram_tensor(name, list(shape), F32, kind="ExternalOutput").ap()
    dbg_outs = {}

    x_own = di("x_own", [TH, D])
    x_prev = di("x_prev", [TH, D])
    x_smp = di("x_smp", [NS, D])
    cpcs = di("cpcs", [17, D])
    consts = di("consts", [128, 1024])
    consts2 = di("consts2", [128, 2048])
    ropetab = di("ropetab", [128, 33, 96])
    flagv = di("flagv", [128, 2])
    erow_d = di("erow_d", [128, 4096])
    state_in = di("state_in", [SB_, 4, 128, 256])
    kvc = [di("kvc0", [SB_, 128, 512]), di("kvc1", [SB_, 512, 512]), di("kvc2", [SB_, 2048, 512])]
    w_ada = di("w_ada", [D, 6 * D])
    b_ada = di("b_ada", [6 * D])
    norm1_g = di("norm1_g", [D])
    norm2_g = di("norm2_g", [D])
    w_in = di("w_in", [D, IN_W])
    b_in = di("b_in", [IN_W])
    w_alpha2 = di("w_alpha2", [16, 512])
    b_alpha2 = di("b_alpha2", [512])
    gla_norm_g = di("gla_norm_g", [256])
    w_proj_a = di("w_proj_a", [D, D])
    w_proj_b = di("w_proj_b", [256, D])
    w_out = di("w_out", [D, D])
    w_up = di("w_up", [D, 2 * DFF])
    w_down = di("w_down", [DFF, D])
    normf_g = di("normf_g", [D])
    y_own = do("y_own", [TH, D])
    y_smp = do("y_smp", [NS, D])
    S_fin = do("S_fin", [4, 128, 256])
    kvp = [do("kvp0", [128, 512]), do("kvp1", [512, 512]), do("kvp2", [2048, 512])]
    S_s = do("S_s", [SB_, 4, 128, 256])
    kvs = [do("kvs0", [SB_, 128, 512]), do("kvs1", [SB_, 512, 512]), do("kvs2", [SB_, 2048, 512])]
    x1_scr = nc.dram_tensor("x1_scr", [TH + NS, D], F32, kind="Internal").ap()
    P.tracked_dram.add("x1_scr")
    g_scr = nc.dram_tensor("g_scr", [4, 128, D], F32, kind="Internal").ap()
    P.tracked_dram.add("g_scr")

    sb = SBAlloc(nc, es, kib=207)
    ps = [es.enter_context(nc.psum_tensor(f"ps{i}", [128, 512], F32)) for i in range(8)]
    psb = [p.bitcast(BF16) for p in ps]
    bank_ctr = [0]
    reserved = set()

    def nbank():
        while True:
            b = bank_ctr[0] % 8
            bank_ctr[0] += 1
            if b not in reserved:
                return b

    def dump(name, ap, shape, dt=F32):
        t = nc.dram_tensor("dbg_" + name, list(shape), dt, kind="ExternalOutput").ap()
        dbg_outs[name] = t
        P.dma("sp", t, ap)

    w_in_v = w_in.rearrange("(k p) n -> p k n", p=128)

    def load_slab(dst, c0, n, src_v=None):
        src_v = w_in_v if src_v is None else src_v
        o = 0
        while o < n:
            m = min(1024, n - o)
            P.dma("pool", dst[:, :, o:o + m], src_v[:, :, c0 + o:c0 + o + m])
            o += m

    def load_row_bf(dst_row, src_1d, c0, n):
        P.dma("pool", dst_row, src_1d[c0:c0 + n].rearrange("(a b) -> a b", a=1))

    def cache_copy(b_):
        if dev is None or dev == "KV":
            for g, W in enumerate((128, 512, 2048)):
                P.dma("act", kvs[g][b_, 0:W - 4, :], kvc[g][b_, 4:W, :])

    cst = sb.alloc([1024], F32)
    P.dma("sp", cst, consts)
    identf = cst[:, C_ID:C_ID + 128]
    selP = cst[0:17, C_SELP:C_SELP + 128]
    selS = cst[0:17, C_SELS:C_SELS + 64]
    ones_f = cst[:, C_ONES:C_ONES + 64]
    trilf = cst[:, C_TRIL:C_TRIL + 128]
    triuf = cst[:, C_TRIU:C_TRIU + 128]
    cb = sb.alloc([512], BF16)
    identb = cb[:, 0:128]
    maskD = cb[:, 128:256]
    maskP = cb[:, 256:384]
    ones_b = cb[:, 384:512]
    P.copy(identb, identf)
    P.copy(cb[:, 128:384], cst[:, C_MD:C_MD + 256])
    P.memset(ones_b, 1.0)
    flg = sb.alloc([2], F32)
    P.dma("sp", flg, flagv)
    epsT = sb.alloc([1], F32)
    P.memset(epsT, EPS)

    P.cur_phase = "M"
    modT = sb.alloc([4, 8, 17], F32)
    A1T = sb.alloc([8, 17], F32)
    A2T = sb.alloc([8, 17], F32)
    S32 = sb.alloc([4, 256], F32)
    mk = sb.mark()
    G1p = sb.alloc([D], F32)
    G2p = sb.alloc([D], F32)
    G1s = sb.alloc([D], F32)
    G2s = sb.alloc([D], F32)
    c_sb = sb.alloc([D], F32)
    c_bf = sb.alloc([D], BF16)
    cT = sb.alloc([8, 17], BF16)
    vecs = sb.alloc([128], F32)
    vecT = sb.alloc([64], F32)
    brow = sb.alloc([2 * D], F32)
    modtok = sb.alloc([2 * D], F32)
    slabs = [sb.alloc([8, D], BF16) for _ in range(2)]

    P.dma("sp", c_sb[0:17, :], cpcs)
    P.dma("sp", vecs[0:48, :], b_ada.rearrange("(a b) -> a b", b=128))
    P.dma("sp", vecs[48:56, :], norm1_g.rearrange("(a b) -> a b", b=128))
    P.dma("sp", vecs[56:64, :], norm2_g.rearrange("(a b) -> a b", b=128))
    P.dma("sp", brow[0:1, 0:D], b_ada[2 * D:3 * D].rearrange("(a b) -> a b", a=1))
    P.dma("sp", brow[0:1, D:2 * D], b_ada[5 * D:6 * D].rearrange("(a b) -> a b", a=1))
    P.act(c_bf[0:17, :], c_sb[0:17, :], AF.Silu)
    for k in range(8):
        P.tr(psb[0][:, k * 32:k * 32 + 17], c_bf[0:17, k * 128:(k + 1) * 128], identb[0:17, 0:17])
    P.copy(cT, psb[0][:, 0:256].rearrange("p (k t) -> p k t", k=8)[:, :, 0:17])
    P.tr(ps[1][:, 0:64], vecs[0:64, :], identf[0:64, 0:64])
    P.copy(vecT, ps[1][:, 0:64])
    badaT = vecT[:, 0:48]
    n1gT = vecT[:, 48:56]
    n2gT = vecT[:, 56:64]
    w_ada_v = w_ada.rearrange("(k p) n -> p k n", p=128)
    fm_slot = {0: 0, 1: 1, 3: 2, 4: 3}
    for si_, s in enumerate((0, 1, 3, 4, 2, 5)):
        slab = slabs[si_ % 2]
        P.dma("pool", slab, w_ada_v[:, :, s * D:(s + 1) * D])
        if s in fm_slot:
            bank = ps[2 + (si_ % 2)]
            for j in range(8):
                for k in range(8):
                    P.mm(bank[:, j * 32:j * 32 + 17], slab[:, k, j * 128:(j + 1) * 128], cT[:, k, :],
                         start=(k == 0), stop=(k == 7))
            P.tt(modT[:, fm_slot[s], :, :], bank[:, 0:256].rearrange("p (j t) -> p j t", j=8)[:, :, 0:17],
                 bcast(badaT[:, 8 * s:8 * s + 8], [[1, 8], [0, 17]]), ALU.add)
            if s == 1:
                P.ts(A1T, modT[:, 1, :, :], 1.0, ALU.add)
                P.tt(A1T, A1T, bcast(n1gT, [[1, 8], [0, 17]]), ALU.mult)
            if s == 4:
                P.ts(A2T, modT[:, 3, :, :], 1.0, ALU.add)
                P.tt(A2T, A2T, bcast(n2gT, [[1, 8], [0, 17]]), ALU.mult)
        else:
            g = 0 if s == 2 else 1
            for hf in range(2):
                bank = ps[4 + hf]
                for k in range(8):
                    P.mm(bank[0:17, :], cT[:, k, :], slab[:, k, hf * 512:(hf + 1) * 512], start=(k == 0), stop=False)
                P.mm(bank[0:17, :], ones_f[0:1, 0:17], brow[0:1, g * D + hf * 512:g * D + (hf + 1) * 512],
                     start=False, stop=True)
                P.copy(modtok[0:17, g * D + hf * 512:g * D + (hf + 1) * 512], bank[0:17, :], eng="act")
    for g, (Gp, Gs) in enumerate(((G1p, G1s), (G2p, G2s))):
        for hf in range(2):
            P.mm(ps[6][:, :], selP, modtok[0:17, g * D + hf * 512:g * D + (hf + 1) * 512])
            P.copy(Gp[:, hf * 512:(hf + 1) * 512], ps[6][:, :], eng="act")
            P.mm(ps[7][0:NS, :], selS, modtok[0:17, g * D + hf * 512:g * D + (hf + 1) * 512])
            P.copy(Gs[0:NS, hf * 512:(hf + 1) * 512], ps[7][0:NS, :], eng="act")
    B1T = modT[:, 0, :, :]
    B2T = modT[:, 2, :, :]
    P.dma("sp", g_scr[0], G1p)
    P.dma("sp", g_scr[1], G2p)
    P.dma("sp", g_scr[2, 0:NS, :], G1s[0:NS, :])
    P.dma("sp", g_scr[3, 0:NS, :], G2s[0:NS, :])
    sb.release(mk)

    def mod_ap(XT, is_smp):
        if is_smp:
            return bcast(XT[:, :, 1:17], [[17, 8], [1, 16], [0, 4]])
        return bcast(XT[:, :, 0:1], [[17, 8], [0, 128]])

    def norm_T(xt, nt, is_smp, AT, BT, out_hT, bufs, i):
        nb_ = bufs.get("nb", 2)
        ssq = bufs["st"][i % nb_]
        P.act(bufs["junk"][0:nt, :], xt[0:nt, :], AF.Square, accum_out=ssq[0:nt, 0:1])
        P.act(ssq[0:nt, 1:2], ssq[0:nt, 0:1], AF.Sqrt, bias=epsT[0:nt, 0:1], scale=1.0 / D)
        P.recip(ssq[0:nt, 2:3], ssq[0:nt, 1:2])
        xn = bufs["xn"][i % nb_]
        P.ts(xn[0:nt, :], xt[0:nt, :], ssq[0:nt, 2:3], ALU.mult)
        bank = psb[nbank()]
        for k in range(8):
            P.tr(bank[:, k * 128:k * 128 + nt], xn[0:nt, k * 128:(k + 1) * 128], identb[0:nt, 0:nt])
        src = bank[:, 0:1024].rearrange("p (k t) -> p k t", k=8)[:, :, 0:nt]
        if is_smp:
            tv = bufs["tmp"][i % nb_][:, :, 0:nt]
            r4 = lambda a: a.rearrange("p k (b s) -> p k b s", s=4)
            P.tt(r4(tv), r4(src), mod_ap(AT, True), ALU.mult)
            P.tt(r4(out_hT), r4(tv), mod_ap(BT, True), ALU.add, eng="pool")
        else:
            for k in range(8):
                if k % 2 == 0 or _os.environ.get("IDEVAC", "0") != "1":
                    P.ts(out_hT[:, k, :], src[:, k, :], AT[:, k, 0:1], ALU.mult, s2=BT[:, k, 0:1], op1=ALU.add)
                else:
                    P.act(out_hT[:, k, :], src[:, k, :], AF.Identity, bias=BT[:, k, 0:1], scale=AT[:, k, 0:1])

    def norm_bufs(nb_=2):
        return dict(xt=[sb.alloc([D], F32) for _ in range(nb_)], st=[sb.alloc([4], F32) for _ in range(nb_)],
                    junk=sb.alloc([D], BF16), xn=[sb.alloc([D], BF16) for _ in range(nb_)],
                    tmp=[sb.alloc([8, NS], F32)] * nb_, nb=nb_)

    def proj_tok(out_ps, lhs_fn, nt, slab, c0, n, brow_bf):
        for k in range(8):
            P.mm(out_ps, lhs_fn(k), slab[:, k, c0:c0 + n], start=(k == 0), stop=False)
        P.mm(out_ps, ones_b[0:1, 0:nt], brow_bf[0:1, c0:c0 + n], start=False, stop=True)

    def rope(Xps, nt, nh, blk, out32, out16, tb):
        X = Xps.rearrange("p (h t d) -> p h t d", h=nh, t=2)
        t1 = tb["t1"][0:nt, 0:nh * 64].rearrange("p (h t d) -> p h t d", h=nh, t=2)
        t2 = tb["t2"][0:nt, 0:nh * 64].rearrange("p (h t d) -> p h t d", h=nh, t=2)
        rt = rt_["t"][0:nt, blk - rt_["b0"], :]
        cc = bcast(rt[:, 0:32], [[0, nh], [0, 2], [1, 32]])
        nsn = bcast(rt[:, 32:64], [[0, nh], [1, 32]])
        psn = bcast(rt[:, 64:96], [[0, nh], [1, 32]])
        P.tt(t1, X, cc, ALU.mult)
        P.tt(t2[:, :, 0, :], X[:, :, 1, :], nsn, ALU.mult)
        P.tt(t2[:, :, 1, :], X[:, :, 0, :], psn, ALU.mult)
        f = lambda a: a.rearrange("p h t d -> p (h t d)")
        if out32 is not None:
            P.tt(out32, f(t1), f(t2), ALU.add, eng="pool")
            if out16 is not None:
                P.copy(out16, out32, eng="pool")
        else:
            P.tt(out16, f(t1), f(t2), ALU.add, eng="pool")

    def gla_gate(hT_fn, nt, slab, c_glr, bglr_col, tb, triU, triL):
        bg = nbank()
        for k in range(8):
            P.mm(ps[bg][0:16, 0:nt], slab[:, k, c_glr:c_glr + 16], hT_fn(k), start=(k == 0), stop=(k == 7))
        P.ts(tb["glrT"][0:16, 0:nt], ps[bg][0:16, 0:nt], bglr_col, ALU.add)
        bz = nbank()
        P.mm(ps[bz][0:nt, :], tb["glrT"][0:16, 0:nt], wal_bf[0:16, :], start=True, stop=False)
        P.mm(ps[bz][0:nt, :], ones_b[0:1, 0:nt], bal_bf[0:1, :], start=False, stop=True)
        P.act(tb["e1"][0:nt, :], ps[bz][0:nt, :], AF.Exp, scale=-1.0)
        P.act(tb["la"][0:nt, :], tb["e1"][0:nt, :], AF.Ln, bias=1.0)
        return tb["la"]

    rt_ = {}
    wal_bf = sb.alloc([512], BF16)
    bal_bf = sb.alloc([512], BF16)
    P.dma("pool", wal_bf[0:16, :], w_alpha2)
    load_row_bf(bal_bf[0:1, :], b_alpha2, 0, 512)
    bvec = sb.alloc([128], F32)
    bcolT = sb.alloc([32], F32)
    P.memset(bvec, 0.0)
    b_in_r = lambda c0, n: b_in[c0:c0 + n].rearrange("(a b) -> a b", b=128)
    P.dma("sp", bvec[0:4, :], b_in_r(O_GQ, 512))
    P.dma("sp", bvec[4:8, :], b_in_r(O_GK, 512))
    P.dma("sp", bvec[8:16, :], b_in_r(O_GA, 1024))
    P.dma("sp", bvec[16:24, :], b_in_r(O_GB, 1024))
    P.dma("sp", bvec[24:25, 0:16], b_in[O_GLR:O_GLR + 16].rearrange("(a b) -> a b", a=1))
    P.tr(ps[0][:, 0:32], bvec[0:32, :], identf[0:32, 0:32])
    P.copy(bcolT, ps[0][:, 0:32])

    hi_mark = sb.hi
    KTp = [sb.alloc([2, 128], BF16, hi=True), sb.alloc([2, 512], BF16, hi=True), sb.alloc([2, 2048], BF16, hi=True)]
    Vfp = [sb.alloc([1, 4, 65], BF16, hi=True), sb.alloc([4, 4, 65], BF16, hi=True), sb.alloc([16, 4, 65], BF16, hi=True)]
    for g in range(3):
        P.memset(Vfp[g], 1.0)

    P.cur_phase = "P"
    mkP = sb.mark()
    SLP = 3088
    slabP = sb.alloc([8, SLP], BF16)
    rt_["t"] = sb.alloc([16, 96], F32)
    rt_["b0"] = 0
    P.dma("sp", rt_["t"], ropetab[:, 0:16, :])
    browP = sb.alloc([SLP], BF16)
    load_slab(slabP[:, :, 0:1536], O_GK, 1536)
    load_slab(slabP[:, :, 1536:3072], O_DK, 1536)
    load_slab(slabP[:, :, 3072:3088], O_GLR, 16)
    load_row_bf(browP[0:1, 0:1536], b_in, O_GK, 1536)
    load_row_bf(browP[0:1, 1536:3072], b_in, O_DK, 1536)
    NPB = 3
    nbufs = norm_bufs(NPB)
    hTp = [sb.alloc([8, 128], BF16) for _ in range(NPB)]
    tbs = [dict(t1=sb.alloc([512], F32), t2=sb.alloc([512], F32), glrT=sb.alloc([128], BF16),
                e1=sb.alloc([512], F32), la=sb.alloc([512], F32), ebs=sb.alloc([512], F32),
                kl=sb.alloc([512], BF16), vbf=sb.alloc([D], BF16), dec=sb.alloc([4], F32),
                kr16=sb.alloc([768], BF16), vn=sb.alloc([3, 4, 65], BF16)) for _ in range(NPB)]
    for t in tbs:
        P.memset(t["vn"], 1.0)
    n_prev = NBLK if dev != "P1" else 2

    def gla_state_update(la, nt, gk_ps, vbf, tb, first, triU, qmask=None):
        bs = nbank()
        P.mm(ps[bs][0:nt, :], triU, la[0:nt, :])
        P.act(tb["ebs"][0:nt, :], ps[bs][0:nt, :], AF.Exp)
        P.tt(tb["kl"][0:nt, :], gk_ps, tb["ebs"][0:nt, :], ALU.mult)

    def p_norm(j):
        xt = nbufs["xt"][j % NPB]
        P.dma("sp", xt, x_prev[j * 128:(j + 1) * 128, :])
        norm_T(xt, 128, False, A1T, B1T, hTp[j % NPB], nbufs, j)

    NA = _os.environ.get("NORM_AHEAD", "1") == "1"
    if NA:
        p_norm(0)
    for j in range(n_prev):
        tb = tbs[j % NPB]
        if NA:
            if j + 1 < n_prev:
                p_norm(j + 1)
        else:
            p_norm(j)
        h = hTp[j % NPB]
        hk = lambda k, h=h: h[:, k, :]
        bk = nbank()
        proj_tok(ps[bk][:, :], hk, 128, slabP, 0, 512, browP)
        bv = [nbank(), nbank()]
        for hf in range(2):
            proj_tok(ps[bv[hf]][:, :], hk, 128, slabP, 512 + hf * 512, 512, browP)
            P.copy(tb["vbf"][:, hf * 512:(hf + 1) * 512], ps[bv[hf]][:, :], eng=("dve" if hf == 0 else "act"))
        la = gla_gate(hk, 128, slabP, 3072, bcolT[0:16, 24:25], tb, triuf, trilf)
        bs = nbank()
        P.mm(ps[bs][:, :], triuf, la)
        P.act(tb["ebs"], ps[bs][:, :], AF.Exp)
        P.tt(tb["kl"], ps[bk][:, :], tb["ebs"], ALU.mult)
        bd = nbank()
        for hh in range(4):
            P.mm(ps[bd][:, hh:hh + 1], la[:, hh * 128:(hh + 1) * 128], cst[:, C_TRIL + 127:C_TRIL + 128])
        P.act(tb["dec"], ps[bd][:, 0:4], AF.Exp)
        for hp in range(2):
            bu = nbank()
            for q in range(2):
                hh = hp * 2 + q
                P.mm(ps[bu][:, q * 256:(q + 1) * 256], tb["kl"][:, hh * 128:(hh + 1) * 128],
                     tb["vbf"][:, hh * 256:(hh + 1) * 256])
            for q in range(2):
                hh = hp * 2 + q
                if j == 0:
                    P.copy(S32[:, hh, :], ps[bu][:, q * 256:(q + 1) * 256])
                else:
                    P.stt(S32[:, hh, :], S32[:, hh, :], tb["dec"][:, hh:hh + 1], ps[bu][:, q * 256:(q + 1) * 256],
                          ALU.mult, ALU.add)
        need = [g for g in range(3) if (g == 2) or (g == 1 and j >= 12) or (g == 0 and j == 15)]
        for g in need:
            dl = DSW_DIL[g]
            bkv = nbank()
            proj_tok(ps[bkv][:, 0:256], hk, 128, slabP, 1536 + g * 256, 256, browP)
            proj_tok(ps[bkv][:, 256:512], hk, 128, slabP, 2304 + g * 256, 256, browP)
            rope(ps[bkv][:, 0:256], 128, 4, j, None, tb["kr16"][:, g * 256:(g + 1) * 256], tb)
            bt = nbank()
            for c in range(2):
                P.tr(psb[bt][:, c * 128:(c + 1) * 128], tb["kr16"][:, g * 256 + c * 128:g * 256 + (c + 1) * 128], identb)
            jj = {0: 0, 1: j - 12, 2: j}[g]
            P.copy(KTp[g][:, :, jj * 128:(jj + 1) * 128], psb[bt][:, 0:256].rearrange("p (c t) -> p c t", c=2), eng="act")
            vsrc = ps[bkv][:, 256:512].rearrange("p (h d) -> p h d", h=4)
            if g == 0:
                P.copy(Vfp[0][:, 0, :, 0:64], vsrc, eng="act")
            else:
                P.copy(tb["vn"][:, g, :, 0:64], vsrc, eng="act")
                np_ = 128 // dl
                for r in range(dl):
                    P.dma("sp", Vfp[g][np_ * jj:np_ * (jj + 1), r, :, :], tb["vn"][r:128:dl, g, :, :])
    P.ts(S32, S32, flg[:, 0:1], ALU.mult)
    if dev in ("P", "P1"):
        dump("S32", S32, [128, 4, 256])
        dump("KTp2", KTp[2], [128, 2, 2048], BF16)
        dump("Vfp2", Vfp[2], [128, 16, 4, 65], BF16)
        dump("KTp0", KTp[0], [128, 2, 128], BF16)
    sb.release(mkP)

    mkPost = sb.mark()
    hT = sb.alloc([8, TH + NS], BF16)
    obT = sb.alloc([4, TH + NS], BF16)
    if dev not in ("P", "P1"):
        P.cur_phase = "N"
        mkN = sb.mark()
        nbufs = norm_bufs()
        for i in range(NBLK + 1):
            nt = 128 if i < NBLK else NS
            xt = nbufs["xt"][i % 2]
            src = x_own[i * 128:(i + 1) * 128, :] if i < NBLK else x_smp
            P.dma("sp", xt[0:nt, :], src)
            norm_T(xt, nt, i == NBLK, A1T, B1T, hT[:, :, i * 128:i * 128 + nt], nbufs, i)
        sb.release(mkN)

        P.cur_phase = "A"
        mkA = sb.mark()
        qs32 = sb.alloc([768], F32)
        ks32 = sb.alloc([768], F32)
        vs32 = sb.alloc([768], F32)
        rt_["t"] = sb.alloc([17, 96], F32)
        rt_["b0"] = 16
        P.dma("sp", rt_["t"], ropetab[:, 16:33, :])
        QsT = sb.alloc([3, 2, NS], BF16)
        KsT = sb.alloc([3, 2, NS], BF16)
        mkA2 = sb.mark()
        OTacc = sb.alloc([4, TH], F32)
        slabA = sb.alloc([8, 768], BF16)
        browA = sb.alloc([768], BF16)
        QT = sb.alloc([2, TH + NS], BF16)
        KT = sb.alloc([2, TH + NS], BF16)
        Vf = sb.alloc([16, 4, 65], BF16)
        P.memset(Vf, 1.0)
        tbs = [dict(t1=sb.alloc([512], F32), t2=sb.alloc([512], F32), q16=sb.alloc([256], BF16),
                    k32=sb.alloc([256], F32), k16=sb.alloc([256], BF16), v32=sb.alloc([256], F32),
                    vn=sb.alloc([4, 65], BF16), pexp=sb.alloc([2, 4, 128], BF16)) for _ in range(2)]
        for t in tbs:
            P.memset(t["vn"], 1.0)
        for g in range(3):
            P.cur_phase = "A"
            dl = DSW_DIL[g]
            np_ = 128 // dl
            for part in range(3):
                load_slab(slabA[:, :, part * 256:(part + 1) * 256], (O_DQ, O_DK, O_DV)[part] + g * 256, 256)
                load_row_bf(browA[0:1, part * 256:(part + 1) * 256], b_in, (O_DQ, O_DK, O_DV)[part] + g * 256, 256)
            for i in range(NBLK + 1):
                nt = 128 if i < NBLK else NS
                smp = i == NBLK
                tb = tbs[i % 2]
                t0 = i * 128
                hk = lambda k, t0=t0, nt=nt: hT[:, k, t0:t0 + nt]
                bqk = nbank()
                proj_tok(ps[bqk][0:nt, :], hk, nt, slabA, 0, 512, browA)
                bvv = nbank()
                proj_tok(ps[bvv][0:nt, 0:256], hk, nt, slabA, 512, 256, browA)
                blk = 16 + i
                if smp:
                    P.copy(vs32[0:nt, g * 256:(g + 1) * 256], ps[bvv][0:nt, 0:256], eng="act")
                    rope(ps[bqk][0:nt, 0:256], nt, 4, blk, qs32[0:nt, g * 256:(g + 1) * 256], tb["q16"][0:nt, :], tb)
                    rope(ps[bqk][0:nt, 256:512], nt, 4, blk, ks32[0:nt, g * 256:(g + 1) * 256], tb["k16"][0:nt, :], tb)
                else:
                    rope(ps[bqk][0:nt, 0:256], nt, 4, blk, None, tb["q16"][0:nt, :], tb)
                    rope(ps[bqk][0:nt, 256:512], nt, 4, blk, tb["k32"][0:nt, :], tb["k16"][0:nt, :], tb)
                bt = nbank()
                for c in range(2):
                    P.tr(psb[bt][:, c * 128:c * 128 + nt], tb["q16"][0:nt, c * 128:(c + 1) * 128], identb[0:nt, 0:nt])
                    P.tr(psb[bt][:, 256 + c * 128:256 + c * 128 + nt], tb["k16"][0:nt, c * 128:(c + 1) * 128], identb[0:nt, 0:nt])
                P.copy(QT[:, :, t0:t0 + nt], psb[bt][:, 0:256].rearrange("p (c t) -> p c t", c=2)[:, :, 0:nt], eng="act")
                P.copy(KT[:, :, t0:t0 + nt], psb[bt][:, 256:512].rearrange("p (c t) -> p c t", c=2)[:, :, 0:nt], eng="act")
                if smp:
                    P.copy(QsT[:, g, :, :], QT[:, :, t0:t0 + nt], eng="pool")
                    P.copy(KsT[:, g, :, :], KT[:, :, t0:t0 + nt], eng="pool")
                    continue
                vsrc = ps[bvv][:, 0:256].rearrange("p (h d) -> p h d", h=4)
                W = (128, 512, 2048)[g]
                if t0 >= TH - W:
                    row0 = t0 - (TH - W)
                    P.dma("sp", kvp[g][row0:row0 + 128, 0:256], tb["k32"])
                    P.copy(tb["v32"], ps[bvv][:, 0:256], eng="act")
                    P.dma("sp", kvp[g][row0:row0 + 128, 256:512], tb["v32"])
                if g == 0:
                    P.copy(Vf[:, i, :, 0:64], vsrc)
                else:
                    P.copy(tb["vn"][:, :, 0:64], vsrc)
                    for r in range(dl):
                        if g == 1:
                            P.dma("sp", Vf[32 * (i % 4):32 * (i % 4 + 1), r * 4 + i // 4, :, :], tb["vn"][r:128:dl, :, :])
                        else:
                            P.dma("sp", Vf[8 * i:8 * (i + 1), r, :, :], tb["vn"][r:128:dl, :, :])
            if dev == "Ap":
                dump("QT", QT, [128, 2, TH + NS], BF16)
                break
            P.cur_phase = "At"
            ti = 0
            for r in range(dl):
                for m in range(16 // dl):
                    tb = tbs[ti % 2]
                    ti += 1
                    q0 = dl * 128 * m + r
                    qsl = slice(q0, q0 + dl * 127 + 1, dl)
                    vidx = {0: m, 1: r * 4 + m, 2: r}[g]
                    tiles = []
                    if m >= 1:
                        p0 = dl * 128 * (m - 1) + r
                        psl = slice(p0, p0 + dl * 127 + 1, dl)
                        tiles.append((KT, psl, Vf[:, {0: m - 1, 1: r * 4 + m - 1, 2: r}[g], :, :], None))
                    else:
                        wprev = (128, 512, 2048)[g]
                        tiles.append((KTp[g], slice(r, wprev, dl), Vfp[g][:, r if g else 0, :, :], flg[:, 1:2]))
                    tiles.append((KT, qsl, Vf[:, vidx, :, :], None))
                    bsx = [nbank(), nbank()]
                    for kt, (Ksrc, ksl, Vt, bias) in enumerate(tiles):
                        for hh in range(4):
                            pp = slice((hh % 2) * 64, (hh % 2) * 64 + 64)
                            col = (kt * 2 + hh // 2) * 128
                            P.mm(ps[bsx[hh % 2]][:, col:col + 128], Ksrc[pp, hh // 2, ksl], QT[pp, hh // 2, qsl])
                    for kt, (Ksrc, ksl, Vt, bias) in enumerate(tiles):
                        for par in range(2):
                            P.act(tb["pexp"][:, kt, par::2, :],
                                  ps[bsx[par]][:, kt * 256:(kt + 1) * 256].rearrange("p (h q) -> p h q", h=2), AF.Exp,
                                  scale=DSW_SCALE, bias=bias)
                        msk = maskP if kt == 0 else maskD
                        P.tt(tb["pexp"][:, kt, :, :], tb["pexp"][:, kt, :, :], bcast(msk, [[0, 4], [1, 128]]), ALU.mult,
                             eng=("pool" if kt == 0 else "dve"))
                    bz = nbank()
                    for hh in range(4):
                        for kt, (Ksrc, ksl, Vt, bias) in enumerate(tiles):
                            P.mm(ps[bz][0:65, hh * 128:(hh + 1) * 128], Vt[:, hh, :], tb["pexp"][:, kt, hh, :],
                                 start=(kt == 0), stop=(kt == 1))
                    src = ps[bz][0:65, :].rearrange("p (h q) -> p h q", h=4)
                    dst = OTacc[0:65, :, qsl]
                    if g == 0:
                        P.copy(dst, src)
                    else:
                        P.tt(dst, src, dst, ALU.add)
            if dev == "At":
                dump("OTacc", OTacc[0:65, :, :], [65, 4, TH])
                break
        if dev not in ("Ap", "At"):
          P.recip(OTacc[64:65, :, :], OTacc[64:65, :, :])
          for hh in range(4):
            for tg in range(4):
                bb = nbank()
                P.mm(ps[bb][0:64, :], ones_f[64:65, 0:64], OTacc[64:65, hh, tg * 512:(tg + 1) * 512])
                P.tt(obT[0:64, hh, tg * 512:(tg + 1) * 512], OTacc[0:64, hh, tg * 512:(tg + 1) * 512], ps[bb][0:64, :], ALU.mult)
        if dev == "A":
            dump("obT", obT[0:64, :, 0:TH], [64, 4, TH], BF16)
            dump("QT", QT, [128, 2, TH + NS], BF16)
            dump("KT", KT, [128, 2, TH + NS], BF16)

        sb.release(mkA2)
        P.cur_phase = "A2"
        if dev not in ("Ap", "At", "A"):
            c2 = sb.alloc([272], F32)
            P.dma("sp", c2, consts2[:, 0:272])
            mask0s = c2[:, 0:4]
            nmask = sb.alloc([3, 64], BF16)
            P.copy(nmask[0:64, :, :], c2[0:64, 64:256].rearrange("p (g t) -> p g t", g=3))
            erow = sb.alloc([64, 64], BF16)
            P.dma("pool", erow.rearrange("p a b -> p (a b)")[:, 0:2048], erow_d[:, 0:2048])
            P.dma("pool", erow.rearrange("p a b -> p (a b)")[:, 2048:4096], erow_d[:, 2048:4096])
            q16s = sb.alloc([768], BF16)
            P.copy(q16s[0:NS, :], qs32[0:NS, :], eng="pool")
            vsx = sb.alloc([3, 4, 65], BF16)
            P.memset(vsx, 1.0)
            P.copy(vsx[0:NS, :, :, 0:64], vs32[0:NS, :].rearrange("p (g h d) -> p g h d", g=3, h=4), eng="pool")
            for g, W in enumerate((128, 512, 2048)):
                for kv_, srcb in enumerate((ks32, vs32)):
                    dst = bass.AP(kvs[g].tensor, (W - 4) * 512 + kv_ * 256, [[W * 512, SB_], [512, 4], [1, 256]])
                    P.dma("sp", dst, srcb[0:NS, g * 256:(g + 1) * 256])
            NCK, NQB = 7, 3
            cks = [sb.alloc([4, 512], F32) for _ in range(NCK)]
            qbs = [sb.alloc([4, 256], F32) for _ in range(NQB)]
            prod = sb.alloc([4, 4, 64], F32)
            scs = [sb.alloc([16], F32) for _ in range(2)]
            pps = [sb.alloc([16], F32) for _ in range(2)]
            tmp2 = [sb.alloc([4, 4, 65], BF16) for _ in range(2)]
            accb = nbank()
            bank_skip = accb
            first = True
            it = 0
            iters = [(b_, g) for b_ in range(SB_) for g in range(3)]
            views = {}

            def a2_front(n_):
                b_, g = iters[n_]
                ck = cks[n_ % NCK]
                qb = qbs[n_ % NQB]
                if g == 0:
                    P.dma("sp", ck[:, 0, :], kvc[0][b_])
                    Kv = bcast(ck[:, 0, 0:256], [[0, 4], [64, 4], [1, 64]])
                    Vv = bcast(ck[:, 0, 256:512], [[0, 4], [64, 4], [1, 64]])
                else:
                    src = kvc[g][b_].rearrange("(i s) e -> i s e", s=DSW_DIL[g])[:, 0:4, :]
                    P.dma("sp", ck, src)
                    Kv = ck[:, :, 0:256].rearrange("p s (h d) -> p s h d", h=4)
                    Vv = ck[:, :, 256:512].rearrange("p s (h d) -> p s h d", h=4)
                views[n_] = (Kv, Vv)
                for sp_ in range(2):
                    bq_ = nbank()
                    if bq_ == bank_skip:
                        bq_ = nbank()
                    for s2 in range(2):
                        s = sp_ * 2 + s2
                        t = 4 * b_ + s
                        P.mm(ps[bq_][:, s2 * 256:(s2 + 1) * 256], bcast(identb[0:NS, t:t + 1], [[0, 128]]),
                             q16s[0:NS, g * 256:(g + 1) * 256])
                    P.copy(qb[:, sp_ * 2:sp_ * 2 + 2, :], ps[bq_][:, :].rearrange("p (s e) -> p s e", s=2), eng="act")

            def a2_back(n_, first):
                b_, g = iters[n_]
                qb = qbs[n_ % NQB]
                sc, pp_, t2 = scs[n_ % 2], pps[n_ % 2], tmp2[n_ % 2]
                Kv, Vv = views[n_]
                P.tt(prod, Kv, qb.rearrange("p s (h d) -> p s h d", h=4), ALU.mult)
                P.op("dve", lambda e, sc=sc: e.tensor_reduce(sc, prod.rearrange("p s h d -> p (s h) d"), AX.X, ALU.add),
                     r=[prod], w=[sc])
                P.act(pp_, sc, AF.Exp, scale=DSW_SCALE)
                if g == 0:
                    P.tt(pp_.rearrange("p (s h) -> p s h", s=4), pp_.rearrange("p (s h) -> p s h", s=4),
                         bcast(mask0s, [[1, 4], [0, 4]]), ALU.mult)
                P.tt(t2[:, :, :, 0:64], Vv, bcast(pp_, [[4, 4], [1, 4], [0, 64]]), ALU.mult, eng="pool")
                P.copy(t2[:, :, :, 64], pp_.rearrange("p (s h) -> p s h", s=4), eng="pool")
                for s in range(4):
                    P.mm(ps[accb][0:NS, 0:260], erow[:, 4 * b_ + s, :], t2[:, s, :, :].rearrange("p h e -> p (h e)"),
                         start=(first and s == 0), stop=False, skip_group_check=True)

            AHEAD = 2
            for n_ in range(min(AHEAD, len(iters))):
                a2_front(n_)
            for n_ in range(len(iters)):
                if n_ + AHEAD < len(iters):
                    a2_front(n_ + AHEAD)
                a2_back(n_, n_ == 0)
            pnew = sb.alloc([12, 64], BF16)
            bsn = [nbank(), nbank()]
            bsn = [x if x != bank_skip else nbank() for x in bsn]
            for g in range(3):
                for hh in range(4):
                    pp = slice((hh % 2) * 64, (hh % 2) * 64 + 64)
                    col = (g * 2 + hh // 2) * 64
                    P.mm(ps[bsn[hh % 2]][0:NS, col:col + 64], KsT[pp, g, hh // 2, :], QsT[pp, g, hh // 2, :])
            for par in range(2):
                P.act(pnew[0:NS, :, :].rearrange("p (g h) t -> p g h t", g=3)[:, :, par::2, :],
                      ps[bsn[par]][0:NS, 0:384].rearrange("p (g h t) -> p g h t", g=3, h=2), AF.Exp, scale=DSW_SCALE)
            P.tt(pnew[0:NS, :, :].rearrange("p (g h) t -> p g h t", g=3), pnew[0:NS, :, :].rearrange("p (g h) t -> p g h t", g=3),
                 bcast(nmask[0:NS, :, :], [[64, 3], [0, 4], [1, 64]]), ALU.mult, eng="pool")
            for g in range(3):
                for hh in range(4):
                    last = (g == 2 and hh == 3)
                    P.mm(ps[accb][0:NS, hh * 65:(hh + 1) * 65], pnew[0:NS, g * 4 + hh, :], vsx[0:NS, g, hh, :],
                         start=False, stop=last, skip_group_check=True)
            accv = ps[accb][0:NS, 0:260].rearrange("p (h e) -> p h e", h=4)
            rden = sb.alloc([4], F32)
            P.recip(rden[0:NS, :], accv[:, :, 64])
            obs = sb.alloc([4, 64], BF16)
            P.tt(obs[0:NS, :, :], accv[:, :, 0:64], bcast(rden[0:NS, :], [[1, 4], [0, 64]]), ALU.mult)
            btr = nbank()
            if btr == bank_skip:
                btr = nbank()
            for hh in range(4):
                P.tr(psb[btr][0:64, hh * 64:(hh + 1) * 64], obs[0:NS, hh, :], identb[0:NS, 0:NS])
            P.copy(obT[0:64, :, TH:TH + NS], psb[btr][0:64, 0:256].rearrange("p (h t) -> p h t", h=4))
            if dev == "A2":
                dump("obTs", obT[0:64, :, TH:TH + NS], [64, 4, NS], BF16)
        sb.release(mkA)
        sb.hi = hi_mark


    P.cur_phase = "Ba"
    if dev not in ("P", "P1", "Ap", "At", "A", "A2"):
        oaT = sb.alloc([8, TH + NS], BF16)
        mkB = sb.mark()
        SLB = 3088
        slabB = sb.alloc([8, SLB], BF16)
        browB = sb.alloc([SLB], BF16)
        load_slab(slabB[:, :, 0:3072], O_GQ, 3072)
        load_slab(slabB[:, :, 3072:3088], O_GLR, 16)
        load_row_bf(browB[0:1, 512:2048], b_in, O_GK, 1536)
        load_row_bf(browB[0:1, 2048:3072], b_in, O_GR, 1024)
        S16 = sb.alloc([4, 256], BF16)
        P.copy(S16, S32, eng="act")
        gng = sb.alloc([256], F32)
        c2b = sb.alloc([80], F32)
        P.dma("sp", c2b[:, 0:64], consts2[:, 64:128])
        P.dma("sp", c2b[:, 64:80], consts2[:, 256:272])
        bcm = sb.alloc([64], BF16)
        P.copy(bcm[0:NS, :], c2b[0:NS, 0:64])
        rowmask = c2b[0:NS, 64:80]
        maskB = sb.alloc([16, 64], BF16)
        mkT = sb.mark()
        gngr = sb.alloc([256], F32)
        P.dma("sp", gngr[0:1, :], gla_norm_g.rearrange("(a b) -> a b", a=1))
        bgn = nbank()
        P.mm(ps[bgn][:, 0:256], cst[0:1, C_SELP:C_SELP + 128], gngr[0:1, :])
        P.copy(gng, ps[bgn][:, 0:256], eng="act")
        maskBf = sb.alloc([1024], F32)
        P.dma("sp", maskBf, consts2[:, 1024:2048])
        P.copy(maskB, maskBf.rearrange("p (b c) -> p b c", b=16), eng="pool")
        sb.release(mkT)
        lnsc = sb.alloc([1], F32)
        P.memset(lnsc, float(np.log(GLA_SCALE)))
        mkW = sb.mark()
        def mk_wk():
            return dict(glrT=sb.alloc([128], BF16), e1=sb.alloc([512], F32), la=sb.alloc([512], F32),
                        ebs=sb.alloc([512], F32), kl=sb.alloc([512], BF16), vbf=sb.alloc([D], BF16),
                        dec=sb.alloc([64], F32), ebT=sb.alloc([4, 128], F32), enbT=sb.alloc([4, 128], F32),
                        qgT=sb.alloc([4, 128], BF16), kdT=sb.alloc([4, 128], BF16), attb=sb.alloc([4, 128], BF16),
                        st=sb.alloc([12], F32), oa=sb.alloc([D], BF16), sg=sb.alloc([D], F32))
        wk = [mk_wk() for _ in range(2)]
        wkh = dict(wk=wk)

        def gla_front(i):
            smp = i == NBLK
            nt = NS if smp else 128
            t0 = i * 128
            wk = wkh["wk"]
            tb = wk[i % len(wk)]
            sg = tb["sg"]
            if i < NBLK and i % 2 == 0:
                cache_copy(i // 2)
            hk = lambda k: hT[:, k, t0:t0 + nt]
            triL = cst[0:NS, C_T4L:C_T4L + NS] if smp else trilf
            triU = cst[0:NS, C_T4U:C_T4U + NS] if smp else triuf
            bq, bkT = nbank(), nbank()
            for which, bnk in ((0, bq), (1, bkT)):
                for hh in range(4):
                    for k in range(8):
                        P.mm(ps[bnk][:, hh * 128:hh * 128 + nt], slabB[:, k, which * 512 + hh * 128:which * 512 + (hh + 1) * 128],
                             hk(k), start=(k == 0), stop=(k == 7))
            bk = nbank()
            proj_tok(ps[bk][0:nt, :], hk, nt, slabB, 512, 512, browB)
            if smp:
                P.memset(tb["vbf"][64:128, :], 0.0)
                P.memset(tb["attb"][64:128, :, :], 0.0)
            for hf in range(2):
                bv = nbank()
                proj_tok(ps[bv][0:nt, :], hk, nt, slabB, 1024 + hf * 512, 512, browB)
                P.copy(tb["vbf"][0:nt, hf * 512:(hf + 1) * 512], ps[bv][0:nt, :], eng=("dve" if hf == 0 else "act"))
            la = gla_gate(hk, nt, slabB, 3072, bcolT[0:16, 24:25], tb, None, None)
            bb = nbank()
            for hh in range(4):
                P.mm(ps[bb][:, hh * 128:hh * 128 + nt], la[0:nt, hh * 128:(hh + 1) * 128], triL[0:nt, 0:nt])
            bTv = ps[bb][:, :].rearrange("p (h c) -> p h c", h=4)[:, :, 0:nt]
            P.act(tb["ebT"][:, :, 0:nt], bTv, AF.Exp, bias=lnsc[:, 0:1])
            P.act(tb["enbT"][:, :, 0:nt], bTv, AF.Exp, scale=-1.0)
            if smp:
                decv = tb["dec"][:, 0:64].rearrange("p (h b) -> p h b", h=4)
                P.act(decv, ps[bb][:, :].rearrange("p (h c) -> p h c", h=4)[:, :, 3:NS:4], AF.Exp)
            else:
                P.act(tb["dec"][:, 0:4], ps[bb][:, 127:512:128], AF.Exp)
            for hh in range(4):
                P.stt(tb["qgT"][:, hh, 0:nt], ps[bq][:, hh * 128:hh * 128 + nt], bcolT[:, hh:hh + 1], tb["ebT"][:, hh, 0:nt], ALU.add, ALU.mult)
                P.stt(tb["kdT"][:, hh, 0:nt], ps[bkT][:, hh * 128:hh * 128 + nt], bcolT[:, 4 + hh:5 + hh], tb["enbT"][:, hh, 0:nt], ALU.add, ALU.mult)
            bs = nbank()
            P.mm(ps[bs][0:nt, :], triU[0:nt, 0:nt], la[0:nt, :])
            P.act(tb["ebs"][0:nt, :], ps[bs][0:nt, :], AF.Exp)
            P.tt(tb["kl"][0:nt, :], ps[bk][0:nt, :], tb["ebs"][0:nt, :], ALU.mult)
            ba = nbank()
            for hh in range(4):
                P.mm(ps[ba][0:nt, hh * 128:hh * 128 + nt], tb["kdT"][:, hh, 0:nt], tb["qgT"][:, hh, 0:nt])
            msk = bcm[0:NS, :] if smp else maskD
            P.tt(tb["attb"][0:nt, :, 0:nt], ps[ba][0:nt, :].rearrange("p (h c) -> p h c", h=4)[:, :, 0:nt],
                 bcast(msk, [[0, 4], [1, nt]]), ALU.mult)
            for hf in range(2):
                bg_ = nbank()
                proj_tok(ps[bg_][0:nt, :], hk, nt, slabB, 2048 + hf * 512, 512, browB)
                P.act(sg[0:nt, hf * 512:(hf + 1) * 512], ps[bg_][0:nt, :], AF.Silu)
            P.tt(sg[0:nt, :].rearrange("p (h v) -> p h v", h=4), sg[0:nt, :].rearrange("p (h v) -> p h v", h=4),
                 bcast(gng[0:nt, :], [[0, 4], [1, 256]]), ALU.mult, eng="pool")

        def gla_back(i):
            smp = i == NBLK
            nt = NS if smp else 128
            t0 = i * 128
            wk = wkh["wk"]
            tb = wk[i % len(wk)]
            sg = tb["sg"]
            junk = tb["oa"]
            bo = [nbank(), nbank()]
            if not smp:
                for hh in range(4):
                    o_ps = ps[bo[hh // 2]][:, (hh % 2) * 256:(hh % 2 + 1) * 256]
                    P.mm(o_ps, tb["attb"][:, hh, :], tb["vbf"][:, hh * 256:(hh + 1) * 256], start=True, stop=False)
                    P.mm(o_ps, tb["qgT"][:, hh, :], S16[:, hh, :], start=False, stop=True)
                for hp in range(2):
                    bu = nbank()
                    for q in range(2):
                        hh = hp * 2 + q
                        P.mm(ps[bu][:, q * 256:(q + 1) * 256], tb["kl"][:, hh * 128:(hh + 1) * 128], tb["vbf"][:, hh * 256:(hh + 1) * 256])
                    for q in range(2):
                        hh = hp * 2 + q
                        P.stt(S32[:, hh, :], S32[:, hh, :], tb["dec"][:, hh:hh + 1], ps[bu][:, q * 256:(q + 1) * 256], ALU.mult, ALU.add)
                P.copy(S16, S32, eng="dve")
            else:
                reserved.update(bo)
                for hh in range(4):
                    o_ps = ps[bo[hh // 2]][0:NS, (hh % 2) * 256:(hh % 2 + 1) * 256]
                    P.mm(o_ps, tb["attb"][:, hh, 0:NS], tb["vbf"][:, hh * 256:(hh + 1) * 256], start=(hh % 2 == 0), stop=False,
                         skip_group_check=True)
                NSF = len(smpb["s0f"])
                NSH = len(smpb["s0h"])

                def s_front(b_):
                    s0f = smpb["s0f"][b_ % NSF]
                    s0h = smpb["s0h"][b_ % NSH]
                    P.dma("sp", s0f, state_in[b_].rearrange("h d v -> d h v"))
                    P.copy(s0h, s0f, eng="act")
                    qgm = smpb["qgm"][b_ % NSH]
                    P.tt(qgm, tb["qgT"][:, :, 0:NS], bcast(maskB[:, b_, :], [[0, 4], [1, NS]]), ALU.mult, eng="pool")
                    klb = smpb["klb"][b_ % NSH]
                    P.ts(klb[0:NS, :], tb["kl"][0:NS, :], rowmask[:, b_:b_ + 1], ALU.mult, eng="dve")

                def s_back(b_):
                    s0f = smpb["s0f"][b_ % NSF]
                    s0h = smpb["s0h"][b_ % NSH]
                    qgm = smpb["qgm"][b_ % NSH]
                    klb = smpb["klb"][b_ % NSH]
                    for hh in range(4):
                        o_ps = ps[bo[hh // 2]][0:NS, (hh % 2) * 256:(hh % 2 + 1) * 256]
                        P.mm(o_ps, qgm[:, hh, :], s0h[:, hh, :], start=False, stop=(b_ == SB_ - 1), skip_group_check=True)
                    so = s0f
                    for hp in range(2):
                        bu = nbank()
                        for q in range(2):
                            hh = hp * 2 + q
                            P.mm(ps[bu][:, q * 256:(q + 1) * 256], klb[0:NS, hh * 128:(hh + 1) * 128], tb["vbf"][0:NS, hh * 256:(hh + 1) * 256])
                        for q in range(2):
                            hh = hp * 2 + q
                            P.stt(so[:, hh, :], s0f[:, hh, :], tb["dec"][:, hh * 16 + b_:hh * 16 + b_ + 1], ps[bu][:, q * 256:(q + 1) * 256],
                                  ALU.mult, ALU.add)
                    P.dma("sp", S_s[b_].rearrange("h d v -> d h v"), so)

                SAH = 2
                for b_ in range(min(SAH, SB_)):
                    s_front(b_)
                for b_ in range(SB_):
                    if b_ + SAH < SB_:
                        s_front(b_ + SAH)
                    s_back(b_)
            for hh in range(4):
                o_ps = ps[bo[hh // 2]][0:nt, (hh % 2) * 256:(hh % 2 + 1) * 256]
                P.act(junk[0:nt, hh * 256:(hh + 1) * 256], o_ps, AF.Square, accum_out=tb["st"][0:nt, hh:hh + 1])
            P.act(tb["st"][0:nt, 4:8], tb["st"][0:nt, 0:4], AF.Sqrt, bias=epsT[0:nt, 0:1], scale=1.0 / 256)
            P.recip(tb["st"][0:nt, 8:12], tb["st"][0:nt, 4:8])
            for hh in range(4):
                o_ps = ps[bo[hh // 2]][0:nt, (hh % 2) * 256:(hh % 2 + 1) * 256]
                P.stt(tb["oa"][0:nt, hh * 256:(hh + 1) * 256], o_ps, tb["st"][0:nt, 8 + hh:9 + hh], sg[0:nt, hh * 256:(hh + 1) * 256],
                      ALU.mult, ALU.mult)
            if smp:
                reserved.difference_update(bo)
            bt = nbank()
            for k in range(8):
                P.tr(psb[bt][:, k * 128:k * 128 + nt], tb["oa"][0:nt, k * 128:(k + 1) * 128], identb[0:nt, 0:nt])
            P.copy(oaT[:, :, t0:t0 + nt], psb[bt][:, 0:1024].rearrange("p (k t) -> p k t", k=8)[:, :, 0:nt], eng="act")

        nb_run = NBLK if dev != "B1" else 2
        gla_front(0)
        for i in range(nb_run):
            if i + 1 < nb_run:
                gla_front(i + 1)
            gla_back(i)
        if dev is None or dev in ("B", "F"):
            P.dma("sp", S_fin.rearrange("h d v -> d h v"), S32)
        sb.release(mkW)
        wkh["wk"] = [mk_wk()]
        smpb = dict(qgm=[sb.alloc([4, NS], BF16) for _ in range(3)], s0f=[sb.alloc([4, 256], F32) for _ in range(4)],
                    s0h=[sb.alloc([4, 256], BF16) for _ in range(3)], klb=[sb.alloc([512], BF16) for _ in range(3)])
        if dev != "B1":
            gla_front(NBLK)
            gla_back(NBLK)
        if dev in ("B", "B1"):
            dump("oaT", oaT, [128, 8, TH + NS], BF16)
            dump("S32", S32, [128, 4, 256])
        sb.release(mkB)

        P.cur_phase = "Bb"
        mergedT = sb.alloc([8, TH + NS], BF16, hi=True)
        mkBb = sb.mark()
        slabG = sb.alloc([8, 2048], BF16)
        wpa = sb.alloc([8, D], BF16)
        wpb = sb.alloc([4, D], BF16)
        load_slab(slabG[:, :, 0:1024], O_GA, 1024)
        load_slab(slabG[:, :, 1024:2048], O_GB, 1024)
        P.dma("pool", wpa, w_proj_a.rearrange("(k p) n -> p k n", p=128))
        P.dma("pool", wpb[0:64, :, :], w_proj_b.rearrange("(h d) n -> d h n", d=64))
        gw = [dict(sga=sb.alloc([512], BF16), sgb=sb.alloc([512], BF16), t1=sb.alloc([512], F32), t2=sb.alloc([512], F32))
              for _ in range(2)]
        it = 0
        for tg in range(5):
            t0 = tg * 512
            ntg = 512 if tg < 4 else NS
            for fc in range(8):
                w_ = gw[it % 2]
                it += 1
                bga, bpa, bgb, bpb = nbank(), nbank(), nbank(), nbank()
                for k in range(8):
                    P.mm(ps[bga][:, 0:ntg], slabG[:, k, fc * 128:(fc + 1) * 128], hT[:, k, t0:t0 + ntg], start=(k == 0), stop=(k == 7))
                P.act(w_["sga"][:, 0:ntg], ps[bga][:, 0:ntg], AF.Sigmoid, bias=bcolT[:, 8 + fc:9 + fc])
                for k in range(8):
                    P.mm(ps[bpa][:, 0:ntg], wpa[:, k, fc * 128:(fc + 1) * 128], oaT[:, k, t0:t0 + ntg], start=(k == 0), stop=(k == 7))
                P.tt(w_["t1"][:, 0:ntg], ps[bpa][:, 0:ntg], w_["sga"][:, 0:ntg], ALU.mult)
                for k in range(8):
                    P.mm(ps[bgb][:, 0:ntg], slabG[:, k, 1024 + fc * 128:1024 + (fc + 1) * 128], hT[:, k, t0:t0 + ntg], start=(k == 0), stop=(k == 7))
                P.act(w_["sgb"][:, 0:ntg], ps[bgb][:, 0:ntg], AF.Sigmoid, bias=bcolT[:, 16 + fc:17 + fc])
                for hh in range(4):
                    P.mm(ps[bpb][:, 0:ntg], wpb[0:64, hh, fc * 128:(fc + 1) * 128], obT[0:64, hh, t0:t0 + ntg], start=(hh == 0), stop=(hh == 3))
                P.tt(w_["t2"][:, 0:ntg], ps[bpb][:, 0:ntg], w_["sgb"][:, 0:ntg], ALU.mult)
                P.tt(mergedT[:, fc, t0:t0 + ntg], w_["t1"][:, 0:ntg], w_["t2"][:, 0:ntg], ALU.add, eng="pool")
        if dev == "Bb":
            dump("mergedT", mergedT, [128, 8, TH + NS], BF16)
        sb.release(mkBb)

    P.cur_phase = "B2"
    if dev is None or dev in ("B2", "F"):
        mkC = sb.mark()
        wout = sb.alloc([8, D], BF16, hi=True)
        P.dma("pool", wout, w_out.rearrange("(k p) n -> p k n", p=128))
        G1p = sb.alloc([D], F32, hi=True)
        G1s = sb.alloc([D], F32, hi=True)
        P.dma("sp", G1p, g_scr[0])
        P.dma("sp", G1s[0:NS, :], g_scr[2, 0:NS, :])
        xb = [sb.alloc([D], F32, hi=True) for _ in range(2)]
        tt_ = [sb.alloc([D], F32, hi=True) for _ in range(2)]
        for i in range(NBLK + 1):
            nt = 128 if i < NBLK else NS
            t0 = i * 128
            xt = xb[i % 2]
            P.dma("sp", xt[0:nt, :], x_own[t0:t0 + 128, :] if i < NBLK else x_smp)
            G1 = G1p if i < NBLK else G1s
            for hf in range(2):
                by = nbank()
                for k in range(8):
                    P.mm(ps[by][0:nt, :], mergedT[:, k, t0:t0 + nt], wout[:, k, hf * 512:(hf + 1) * 512], start=(k == 0), stop=(k == 7))
                P.tt(tt_[i % 2][0:nt, hf * 512:(hf + 1) * 512], ps[by][0:nt, :], G1[0:nt, hf * 512:(hf + 1) * 512], ALU.mult)
            P.tt(xt[0:nt, :], xt[0:nt, :], tt_[i % 2][0:nt, :], ALU.add, eng="pool")
            P.dma("sp", x1_scr[t0:t0 + nt, :], xt[0:nt, :])
        sb.release(mkC)
        sb.top = mkPost
        sb.hi = hi_mark

        P.cur_phase = "C"
        wup = sb.alloc([8, 2 * DFF], BF16)
        wdn = sb.alloc([22, D], BF16)
        w_up_v = w_up.rearrange("(k p) n -> p k n", p=128)
        for o_ in range(0, 2 * DFF, 1024):
            m_ = min(1024, 2 * DFF - o_)
            P.dma("pool", wup[:, :, o_:o_ + m_], w_up_v[:, :, o_:o_ + m_])
        w_dn_v = w_down.rearrange("(k p) n -> p k n", p=128)
        for k0 in range(0, 22, 8):
            k1 = min(22, k0 + 8)
            P.dma("pool", wdn[:, k0:k1, :], w_dn_v[:, k0:k1, :])
        G2p = sb.alloc([D], F32)
        G2s = S32.rearrange("p h v -> p (h v)")
        P.dma("sp", G2p, g_scr[1])
        P.dma("sp", G2s[0:NS, :], g_scr[3, 0:NS, :])
        nfg = sb.alloc([D], F32)
        mkT2 = sb.mark()
        nfr = sb.alloc([D], F32)
        P.dma("sp", nfr[0:1, :], normf_g.rearrange("(a b) -> a b", a=1))
        for hf in range(2):
            bn_ = nbank()
            P.mm(ps[bn_][:, :], cst[0:1, C_SELP:C_SELP + 128], nfr[0:1, hf * 512:(hf + 1) * 512])
            P.copy(nfg[:, hf * 512:(hf + 1) * 512], ps[bn_][:, :], eng="act")
        sb.release(mkT2)
        xn_b = [sb.alloc([D], BF16) for _ in range(2)]
        nbufs = dict(xt=[sb.alloc([D], F32) for _ in range(2)], st=[sb.alloc([4], F32) for _ in range(2)],
                     junk=None, xn=xn_b, tmp=[cst[:, 0:8 * NS].rearrange("p (k t) -> p k t", k=8)] * 2)
        xr = [sb.alloc([D], F32) for _ in range(2)]
        ttb = cst
        h2g = [sb.alloc([8, 256], BF16) for _ in range(2)]
        actTs = [sb.alloc([22, 256], BF16) for _ in range(2)]
        su = [sb.alloc([256], BF16) for _ in range(2)]
        stf = [sb.alloc([4], F32) for _ in range(2)]
        groups = [[2 * g_, 2 * g_ + 1] for g_ in range(NBLK // 2)] + [[NBLK]]
        ginfo = {}

        def c_norm(gi):
            hb = h2g[gi % 2]
            ntg = 0
            offs = []
            for i in groups[gi]:
                nt = 128 if i < NBLK else NS
                xt = nbufs["xt"][i % 2]
                P.dma("sp", xt[0:nt, :], x1_scr[i * 128:i * 128 + nt, :])
                nbufs["junk"] = xn_b[i % 2]
                norm_T(xt, nt, i == NBLK, A2T, B2T, hb[:, :, ntg:ntg + nt], nbufs, i)
                offs.append((i, ntg, nt))
                ntg += nt
            ginfo[gi] = (offs, ntg)

        if NA:
            c_norm(0)
        for gi, blks in enumerate(groups):
            hb = h2g[gi % 2]
            actT = actTs[gi % 2]
            if NA:
                if gi + 1 < len(groups):
                    c_norm(gi + 1)
            else:
                c_norm(gi)
            offs, ntg = ginfo[gi]
            if 1 <= gi <= 8:
                cache_copy(8 + gi - 1)
            for fc in range(22):
                b1, b2 = nbank(), nbank()
                for k in range(8):
                    P.mm(ps[b1][:, 0:ntg], wup[:, k, fc * 128:(fc + 1) * 128], hb[:, k, 0:ntg], start=(k == 0), stop=(k == 7))
                for k in range(8):
                    P.mm(ps[b2][:, 0:ntg], wup[:, k, DFF + fc * 128:DFF + (fc + 1) * 128], hb[:, k, 0:ntg], start=(k == 0), stop=(k == 7))
                s_ = su[fc % 2]
                P.act(s_[:, 0:ntg], ps[b1][:, 0:ntg], AF.Silu)
                P.tt(actT[:, fc, 0:ntg], ps[b2][:, 0:ntg], s_[:, 0:ntg], ALU.mult)
            for (i, o_, nt) in offs:
                smp = i == NBLK
                G2 = G2s if smp else G2p
                x1r = xr[i % 2]
                P.dma("sp", x1r[0:nt, :], x1_scr[i * 128:i * 128 + nt, :])
                for hf in range(2):
                    by = nbank()
                    for kc in range(22):
                        P.mm(ps[by][0:nt, :], actT[:, kc, o_:o_ + nt], wdn[:, kc, hf * 512:(hf + 1) * 512], start=(kc == 0), stop=(kc == 21))
                    P.tt(ttb[0:nt, hf * 512:(hf + 1) * 512], ps[by][0:nt, :], G2[0:nt, hf * 512:(hf + 1) * 512], ALU.mult)
                P.tt(x1r[0:nt, :], x1r[0:nt, :], ttb[0:nt, :], ALU.add, eng="pool")
                st = stf[i % 2]
                P.act(ttb[0:nt, :], x1r[0:nt, :], AF.Square, accum_out=st[0:nt, 0:1])
                P.act(st[0:nt, 1:2], st[0:nt, 0:1], AF.Sqrt, bias=epsT[0:nt, 0:1], scale=1.0 / D)
                P.recip(st[0:nt, 2:3], st[0:nt, 1:2])
                P.stt(x1r[0:nt, :], x1r[0:nt, :], st[0:nt, 2:3], nfg[0:nt, :], ALU.mult, ALU.mult)
                P.dma("sp", y_own[i * 128:(i + 1) * 128, :] if i < NBLK else y_smp, x1r[0:nt, :])

    P.finalize()
    es.close()
    return nc, P, dbg_outs


def make_consts():
    c = np.zeros((128, 1024), np.float32)
    c[:, C_ID:C_ID + 128] = np.eye(128, dtype=np.float32)
    c[0, C_SELP:C_SELP + 128] = 1.0
    for b in range(16):
        c[1 + b, C_SELS + 4 * b:C_SELS + 4 * b + 4] = 1.0
    c[:, C_ONES:C_ONES + 64] = 1.0
    i = np.arange(128)
    c[:, C_TRIL:C_TRIL + 128] = np.where(i[:, None] <= i[None, :], -1.0 / 16, 0.0)
    c[:, C_TRIU:C_TRIU + 128] = np.where(i[:, None] > i[None, :], -1.0 / 16, 0.0)
    c[:, C_MD:C_MD + 128] = (i[:, None] <= i[None, :]).astype(np.float32)
    c[:, C_MP:C_MP + 128] = (i[:, None] >= i[None, :]).astype(np.float32)
    j = np.arange(64)
    same = (j[:, None] // 4) == (j[None, :] // 4)
    c[0:64, C_T4L:C_T4L + 64] = np.where(same & (j[:, None] <= j[None, :]), -1.0 / 16, 0.0)
    c[0:64, C_T4U:C_T4U + 64] = np.where(same & (j[:, None] > j[None, :]), -1.0 / 16, 0.0)
    return c


def make_consts2():
    c = np.zeros((128, 2048), np.float32)
    i = np.arange(128)
    c[:, 0:4] = (i[:, None] >= np.arange(4)[None, :]).astype(np.float32)
    j = np.arange(64)
    same = (j[:, None] // 4) == (j[None, :] // 4)
    c[0:64, 64:128] = (same & (j[:, None] <= j[None, :])).astype(np.float32)
    c[0:64, 128:192] = np.eye(64, dtype=np.float32)
    c[0:64, 192:256] = np.eye(64, dtype=np.float32)
    c[0:64, 256:272] = (j[:, None] // 4 == np.arange(16)[None, :]).astype(np.float32)
    mb = (np.arange(64)[None, :] // 4 == np.arange(16)[:, None]).astype(np.float32)
    c[:, 1024:2048] = mb.reshape(1, 1024)
    return c


def make_erow():
    e = np.zeros((128, 64, 64), np.float32)
    for t in range(64):
        e[:, t, t] = 1.0
    return e.reshape(128, 4096)


def make_ropetab(half):
    inv = (np.float32(10000.0) ** (-(np.arange(32, dtype=np.float32)) / np.float32(32))).astype(np.float32)
    tab = np.zeros((128, 33, 96), np.float32)
    p = np.arange(128)
    for blk in range(33):
        if blk < 16:
            pos = (half - 1) * TH + blk * 128 + p
        elif blk < 32:
            pos = half * TH + (blk - 16) * 128 + p
        else:
            pos = 8192 + (p % 4)
        ang = (pos.astype(np.float32)[:, None] * inv[None, :]).astype(np.float32)
        cs, sn = np.cos(ang).astype(np.float32), np.sin(ang).astype(np.float32)
        tab[:, blk, 0:32] = cs
        tab[:, blk, 32:64] = -sn
        tab[:, blk, 64:96] = sn
    return tab


def make_in_maps(inputs, cores, sample_base=None):
    consts = make_consts()
    consts2 = make_consts2()
    erow = make_erow()
    maps = []
    for ci, c in enumerate(cores):
        b, half = c // 2, c % 2
        xp = inputs["x_prompt"]
        s0 = SB_ * c
        flagv = np.zeros((128, 2), np.float32)
        flagv[:, 0] = float(half)
        flagv[:, 1] = (float(half) - 1.0) * 30000.0
        m = {
            "x_own": np.ascontiguousarray(xp[b, half * TH:(half + 1) * TH]),
            "x_prev": np.ascontiguousarray(xp[b, 0:TH]) if half == 1 else np.zeros((TH, D), np.float32),
            "x_smp": np.ascontiguousarray(inputs["x_sample"][s0:s0 + SB_].reshape(NS, D)),
            "cpcs": np.ascontiguousarray(np.concatenate([inputs["c_prompt"][b:b + 1], inputs["c_sample"][s0:s0 + SB_]], 0)),
            "consts": consts, "consts2": consts2, "ropetab": make_ropetab(half), "flagv": flagv, "erow_d": erow,
            "state_in": np.ascontiguousarray(inputs["state_gla"][0, s0:s0 + SB_]),
            "kvc0": np.ascontiguousarray(inputs["cache_kv_w128"][0, s0:s0 + SB_].reshape(SB_, 128, 512)),
            "kvc1": np.ascontiguousarray(inputs["cache_kv_w512"][0, s0:s0 + SB_].reshape(SB_, 512, 512)),
            "kvc2": np.ascontiguousarray(inputs["cache_kv_w2048"][0, s0:s0 + SB_].reshape(SB_, 2048, 512)),
            "normf_g": np.ascontiguousarray(inputs["normf_g"]),
        }
        for k in ("w_ada", "b_ada", "norm1_g", "norm2_g", "w_in", "b_in", "w_alpha2", "b_alpha2", "gla_norm_g",
                  "w_proj_a", "w_proj_b", "w_out", "w_up", "w_down"):
            m[k] = np.ascontiguousarray(inputs[k][0])
        maps.append(m)
    return maps


_CACHE = {}


def kernel(**inputs):
    inputs = {k: np.asarray(v) for k, v in inputs.items()}
    if "nc" not in _CACHE:
        _CACHE["nc"] = build()[0]
    nc = _CACHE["nc"]
    cores = list(range(NCORES))
    maps = make_in_maps(inputs, cores)
    res = run_bass_kernel_spmd(nc, maps, core_ids=cores).results
    return assemble(res, cores)


def assemble(res, cores):
    nb = max(c // 2 for c in cores) + 1
    nsb = SB_ * len(cores)
    y_p = np.zeros((nb, 2 * TH, D), np.float32)
    y_s = np.zeros((nsb, 4, D), np.float32)
    S_p = np.zeros((1, nb, 4, 128, 256), np.float32)
    kvp = [np.zeros((1, nb, W, 2, 4, 64), np.float32) for W in (128, 512, 2048)]
    S_s = np.zeros((1, nsb, 4, 128, 256), np.float32)
    kvs = [np.zeros((1, nsb, W, 2, 4, 64), np.float32) for W in (128, 512, 2048)]
    for ci, c in enumerate(cores):
        r = res[ci]
        b, half = c // 2, c % 2
        y_p[b, half * TH:(half + 1) * TH] = r["y_own"]
        y_s[SB_ * ci:SB_ * (ci + 1)] = np.asarray(r["y_smp"]).reshape(SB_, 4, D)
        S_s[0, SB_ * ci:SB_ * (ci + 1)] = r["S_s"]
        for g, W in enumerate((128, 512, 2048)):
            kvs[g][0, SB_ * ci:SB_ * (ci + 1)] = np.asarray(r[f"kvs{g}"]).reshape(SB_, W, 2, 4, 64)
        if half == 1:
            S_p[0, b] = r["S_fin"]
            for g, W in enumerate((128, 512, 2048)):
                kvp[g][0, b] = np.asarray(r[f"kvp{g}"]).reshape(W, 2, 4, 64)
    return (y_p, y_s, S_p, kvp[0], kvp[1], kvp[2], S_s, kvs[0], kvs[1], kvs[2])
```

```python
import numpy as np
from contextlib import ExitStack
import concourse.bass as bass
import concourse.mybir as mybir
from concourse.bass_utils import run_bass_kernel_spmd

F32 = mybir.dt.float32
BF16 = mybir.dt.bfloat16
AF = mybir.ActivationFunctionType
ALU = mybir.AluOpType
AX = mybir.AxisListType

ENGS = ("pe", "act", "dve", "pool", "sp")
import os as _os
REORDER_ENGS = tuple(_os.environ.get("REORDER_ENGS", "pe,dve,pool,sp").split(","))
D = 1024
NCORES = 8
TH = 2048
NBLK = 16
SB_ = 16
NS = 64
EPS = 1e-6
IN_W = 7440
O_GQ, O_GK, O_GV, O_GR, O_GLR, O_DQ, O_DK, O_DV, O_GA, O_GB = 0, 512, 1024, 2048, 3072, 3088, 3856, 4624, 5392, 6416
DFF = 2816
SB_GRAN = 256
PS_GRAN = 2048
DR_GRAN = 1 << 14


def _esize(dt):
    return mybir.dt.size(dt)


class Prog:
    N_DMA_SEMS = 40
    N_SW_SEMS = 16

    def __init__(self, nc):
        self.nc = nc
        self.ops = []
        self.tracked_dram = set()
        self.cur_phase = "init"

    def _gran(self, ap):
        t = ap.tensor
        es = _esize(ap.dtype)
        sp = str(t.space)
        dims = list(ap.ap)
        if "DRAM" in sp:
            if t.name not in self.tracked_dram:
                return ()
            lo = ap.offset
            ext = sum((c - 1) * abs(s) for s, c in dims) + 1
            g0, g1 = (lo * es) // DR_GRAN, ((lo + ext) * es - 1) // DR_GRAN
            return [(t.name, g) for g in range(g0, g1 + 1)]
        rowlen = t.shape[1]
        for d in t.shape[2:]:
            rowlen *= d
        col0 = ap.offset % rowlen
        ext = sum((c - 1) * abs(s) for s, c in dims[1:]) + 1
        gr = PS_GRAN if "PSUM" in sp else SB_GRAN
        g0, g1 = (col0 * es) // gr, ((col0 + ext) * es - 1) // gr
        return [(t.name, g) for g in range(g0, g1 + 1)]

    def op(self, eng, fn, r=(), w=(), dma=False):
        rk, wk = set(), set()
        for a in r:
            if a is not None and not isinstance(a, (int, float)):
                rk.update(self._gran(a))
        for a in w:
            wk.update(self._gran(a))
        n_el, nbytes, f32 = 1, 0, 1
        if len(w) > 0:
            n_el = 1
            for d_ in w[0].shape[1:]:
                n_el *= int(d_)
            nbytes = n_el * int(w[0].shape[0]) * _esize(w[0].dtype)
        if eng == "pe" and len(r) > 1:
            n_el = 1
            for d_ in r[1].shape[1:]:
                n_el *= int(d_)
            f32 = 4 if r[1].dtype == F32 else 1
        self.ops.append(dict(eng=eng, fn=fn, r=rk, w=wk, dma=dma, n=n_el, bytes=nbytes, f32=f32, ph=self.cur_phase))

    def mm(self, out, lhsT, rhs, start=True, stop=True, **kw):
        self.op("pe", lambda e: e.matmul(out, lhsT, rhs, start=start, stop=stop, **kw),
                r=[lhsT, rhs] + ([] if start else [out]), w=[out])

    def tr(self, out, in_, ident):
        self.op("pe", lambda e: e.transpose(out, in_, ident), r=[in_, ident], w=[out])

    def act(self, out, in_, func, bias=None, scale=None, accum_out=None, eng="act"):
        kw = {}
        if bias is not None:
            kw["bias"] = bias
        if scale is not None:
            kw["scale"] = scale
        if accum_out is not None:
            kw["accum_out"] = accum_out
        self.op(eng, lambda e: e.activation(out, in_, func, **kw), r=[in_, bias, scale],
                w=[out] + ([accum_out] if accum_out is not None else []))

    def tt(self, out, in0, in1, op, eng="dve"):
        self.op(eng, lambda e: e.tensor_tensor(out, in0, in1, op), r=[in0, in1], w=[out])

    def ts(self, out, in0, s1, op0, s2=None, op1=None, eng="dve", accum_out=None):
        kw = {}
        if op1 is not None:
            kw["op1"] = op1
        if accum_out is not None:
            kw["accum_out"] = accum_out
        self.op(eng, lambda e: e.tensor_scalar(out, in0, s1, s2, op0, **kw), r=[in0, s1, s2],
                w=[out] + ([accum_out] if accum_out is not None else []))

    def stt(self, out, in0, scalar, in1, op0, op1):
        self.op("dve", lambda e: e.scalar_tensor_tensor(out, in0, scalar, in1, op0, op1),
                r=[in0, scalar, in1], w=[out])

    def copy(self, out, in_, eng="dve"):
        if eng == "act":
            self.op("act", lambda e: e.copy(out, in_), r=[in_], w=[out])
        else:
            self.op(eng, lambda e: e.tensor_copy(out, in_), r=[in_], w=[out])

    def memset(self, ap, val, eng="pool"):
        self.op(eng, lambda e: e.memset(ap, val), w=[ap])

    def recip(self, out, in_):
        self.op("dve", lambda e: e.reciprocal(out, in_), r=[in_], w=[out])

    def dma(self, q, out, in_, **kw):
        self.op(q, lambda e: e.dma_start(out, in_, **kw), r=[in_], w=[out], dma=True)

    def _cost(self, o):
        n, e = o["n"], o["eng"]
        if o["dma"]:
            return 60.0
        if e == "pe":
            return 60.0 + max(56.0, 0.42 * n * o.get("f32", 1))
        if e == "act":
            return 220.0 + 0.75 * n
        if e == "dve":
            return 130.0 + 1.05 * n
        return 320.0 + 1.3 * n

    def finalize(self, window=int(_os.environ.get("RWINDOW", "128")), reorder=True, reorder_engs=REORDER_ENGS):
        nc, ops = self.nc, self.ops
        n = len(ops)
        last_w, readers = {}, {}
        deps = [set() for _ in range(n)]
        for j, o in enumerate(ops):
            dj = deps[j]
            for k in o["r"]:
                i = last_w.get(k)
                if i is not None:
                    dj.add(i)
            for k in o["w"]:
                i = last_w.get(k)
                if i is not None:
                    dj.add(i)
                rs = readers.get(k)
                if rs:
                    dj.update(rs)
            for k in o["r"]:
                readers.setdefault(k, []).append(j)
            for k in o["w"]:
                last_w[k] = j
                readers[k] = []
            dj.discard(j)
        self.gaps = {}
        pend = {e: [j for j, o in enumerate(ops) if o["eng"] == e] for e in ENGS}
        head = {e: 0 for e in ENGS}
        free_t = {e: 0.0 for e in ENGS}
        fin = [None] * n
        start_t = [0.0] * n
        order = {e: [] for e in ENGS}
        done_cnt = 0
        taken = [False] * n
        while done_cnt < n:
            best = None
            for e in ENGS:
                lst = pend[e]
                h = head[e]
                while h < len(lst) and taken[lst[h]]:
                    h += 1
                head[e] = h
                cnt = 0
                k = h
                while k < len(lst) and cnt < (window if (reorder and e in reorder_engs) else 1):
                    j = lst[k]
                    k += 1
                    if taken[j]:
                        continue
                    cnt += 1
                    rdy = 0.0
                    ok = True
                    who = -1
                    for i in deps[j]:
                        f = fin[i]
                        if f is None:
                            ok = False
                            break
                        if f > rdy:
                            rdy = f
                            who = i
                    if not ok:
                        continue
                    st = max(rdy, free_t[e])
                    if best is None or st < best[0] or (st == best[0] and j < best[1]):
                        best = (st, j, e, who if rdy > free_t[e] else -1, max(0.0, rdy - free_t[e]))
            assert best is not None, "scheduler deadlock"
            st, j, e, who_, gap_ = best
            o = ops[j]
            if who_ >= 0:
                key_ = (o["ph"], e, ops[who_]["eng"] + ("/dma" if ops[who_]["dma"] else ""))
                self.gaps[key_] = self.gaps.get(key_, 0.0) + gap_
            c = self._cost(o)
            start_t[j] = st
            free_t[e] = st + c
            fin[j] = st + c + ((2000.0 + o["bytes"] / 150.0) if o["dma"] else 60.0)
            taken[j] = True
            order[e].append(j)
            done_cnt += 1
        self.sim_time_us = max(f for f in fin) / 1000.0
        rep = {}
        for j, o in enumerate(ops):
            d = rep.setdefault(o["ph"], dict(t0=1e18, t1=0.0, busy={e: 0.0 for e in ENGS}))
            d["t0"] = min(d["t0"], start_t[j])
            d["t1"] = max(d["t1"], fin[j])
            d["busy"][o["eng"]] += self._cost(o)
        self.phase_report = {k: (round(v["t0"] / 1000), round(v["t1"] / 1000), {e: round(b / 1000) for e, b in v["busy"].items()}) for k, v in rep.items()}
        pos = {}
        for e in ENGS:
            for p_, j in enumerate(order[e]):
                pos[j] = p_
        pools = {"sp": (0, 32), "act": (32, 8), "pool": (40, 16)}
        nsem = 56
        dma_sem_of, dma_val_of, sem_last, sem_cnt = {}, {}, {}, {}
        for e in ENGS:
            rr = 0
            for j in order[e]:
                if ops[j]["dma"]:
                    base, cnt_ = pools[e]
                    s_ = base + rr % cnt_
                    rr += 1
                    if s_ in sem_last:
                        deps[j].add(sem_last[s_])
                    sem_last[s_] = j
                    sem_cnt[s_] = sem_cnt.get(s_, 0) + 1
                    dma_sem_of[j] = s_
                    dma_val_of[j] = 16 * sem_cnt[s_]
        need_inc = [False] * n
        final_deps = [[] for _ in range(n)]
        for j, o in enumerate(ops):
            best = {}
            for i in deps[j]:
                oi = ops[i]
                if not oi["dma"] and not o["dma"] and oi["eng"] == o["eng"] and o["eng"] == "pe":
                    continue
                if oi["dma"]:
                    final_deps[j].append(i)
                    need_inc[i] = True
                else:
                    b_ = best.get(oi["eng"])
                    if b_ is None or pos[b_] < pos[i]:
                        best[oi["eng"]] = i
            for i in best.values():
                final_deps[j].append(i)
                need_inc[i] = True
        val_of = {}
        cnt = {e: 0 for e in ENGS}
        for e in ENGS:
            for j in order[e]:
                if not ops[j]["dma"] and need_inc[j]:
                    cnt[e] += 1
                    val_of[j] = cnt[e]
        self.stats = dict(n_ops=n, cnt=dict(cnt), n_dma=len(dma_sem_of), sim_us=round(self.sim_time_us))
        es = ExitStack()
        esem = {e: es.enter_context(nc.semaphore(f"c_{e}")) for e in ENGS}
        dsem = [es.enter_context(nc.semaphore(f"d_{i}")) for i in range(nsem)]

        def emit_engine(ename, eng):
            waited = {}
            for j in order[ename]:
                o = ops[j]
                for i in sorted(final_deps[j]):
                    oi = ops[i]
                    if oi["dma"]:
                        key, sem, v = ("d", dma_sem_of[i]), dsem[dma_sem_of[i]], dma_val_of[i]
                    else:
                        key, sem, v = ("c", oi["eng"]), esem[oi["eng"]], val_of[i]
                    if waited.get(key, 0) >= v:
                        continue
                    waited[key] = v
                    eng.wait_ge(sem, v)
                ins = o["fn"](eng)
                if o["dma"]:
                    ins.then_inc(dsem[dma_sem_of[j]], 16)
                elif need_inc[j]:
                    ins.then_inc(esem[ename], 1)
            if ename == "sp":
                for s_, c in sem_cnt.items():
                    if waited.get(("d", s_), 0) < 16 * c:
                        eng.wait_ge(dsem[s_], 16 * c)
                for e2 in ("pe", "act", "dve", "pool"):
                    if cnt[e2] > 0:
                        eng.wait_ge(esem[e2], cnt[e2])

        with nc.Block() as block:
            @block.tensor
            def _(e):
                emit_engine("pe", e)

            @block.scalar
            def _(e):
                emit_engine("act", e)

            @block.vector
            def _(e):
                emit_engine("dve", e)

            @block.gpsimd
            def _(e):
                emit_engine("pool", e)

            @block.sync
            def _(e):
                emit_engine("sp", e)
        es.close()


class SBAlloc:
    def __init__(self, nc, es, kib=200):
        self.nbytes = kib * 1024
        self.t32 = es.enter_context(nc.sbuf_tensor("SB", [128, self.nbytes // 4], F32))
        self.t16 = self.t32.bitcast(BF16)
        self.top = 0
        self.hi = self.nbytes
        self.peak = 0

    def mark(self):
        return self.top

    def release(self, m):
        self.top = m

    def alloc(self, shape, dt, hi=False):
        es = _esize(dt)
        nel = int(np.prod(shape))
        nb = (nel * es + 63) // 64 * 64
        if hi:
            self.hi -= nb
            off = self.hi
        else:
            off = self.top
            self.top += nb
        self.peak = max(self.peak, self.top + self.nbytes - self.hi)
        assert self.top <= self.hi, f"SBUF overflow {self.top} {self.hi}"
        base = self.t32 if dt == F32 else self.t16
        ap = base[:, off // es: off // es + nel]
        if len(shape) > 1:
            names = " ".join(f"d{i}" for i in range(len(shape)))
            ap = ap.rearrange(f"p ({names}) -> p {names}", **{f"d{i}": int(s) for i, s in enumerate(shape)})
        return ap


def bcast(ap, shape_pattern):
    pstride = ap.ap[0][0]
    return bass.AP(ap.tensor, ap.offset, [[pstride, ap.ap[0][1]]] + [list(x) for x in shape_pattern])


GLA_SCALE = 128 ** -0.5
DSW_SCALE = 0.125
DSW_DIL = (1, 4, 16)
C_ID, C_SELP, C_SELS, C_ONES, C_TRIL, C_TRIU, C_MD, C_MP, C_T4L, C_T4U = 0, 128, 256, 320, 384, 512, 640, 768, 896, 960


def build(dev=None):
    nc = bass.Bass("TRN2", target_bir_lowering=False)
    es = ExitStack()
    P = Prog(nc)
    di = lambda name, shape: nc.dram_tensor(name, list(shape), F32, kind="ExternalInput").ap()
    do = lambda name, shape: nc.dram_tensor(name, list(shape), F32, kind="ExternalOutput").ap()
    dbg_outs = {}

    x_own = di("x_own", [TH, D])
    x_prev = di("x_prev", [TH, D])
    x_smp = di("x_smp", [NS, D])
    cpcs = di("cpcs", [17, D])
    consts = di("consts", [128, 1024])
    consts2 = di("consts2", [128, 2048])
    ropetab = di("ropetab", [128, 33, 96])
    flagv = di("flagv", [128, 2])
    erow_d = di("erow_d", [128, 4096])
    state_in = di("state_in", [SB_, 4, 128, 256])
    kvc = [di("kvc0", [SB_, 128, 512]), di("kvc1", [SB_, 512, 512]), di("kvc2", [SB_, 2048, 512])]
    w_ada = di("w_ada", [D, 6 * D])
    b_ada = di("b_ada", [6 * D])
    norm1_g = di("norm1_g", [D])
    norm2_g = di("norm2_g", [D])
    w_in = di("w_in", [D, IN_W])
    b_in = di("b_in", [IN_W])
    w_alpha2 = di("w_alpha2", [16, 512])
    b_alpha2 = di("b_alpha2", [512])
    gla_norm_g = di("gla_norm_g", [256])
    w_proj_a = di("w_proj_a", [D, D])
    w_proj_b = di("w_proj_b", [256, D])
    w_out = di("w_out", [D, D])
    w_up = di("w_up", [D, 2 * DFF])
    w_down = di("w_down", [DFF, D])
    normf_g = di("normf_g", [D])
    y_own = do("y_own", [TH, D])
    y_smp = do("y_smp", [NS, D])
    S_fin = do("S_fin", [4, 128, 256])
    kvp = [do("kvp0", [128, 512]), do("kvp1", [512, 512]), do("kvp2", [2048, 512])]
    S_s = do("S_s", [SB_, 4, 128, 256])
    kvs = [do("kvs0", [SB_, 128, 512]), do("kvs1", [SB_, 512, 512]), do("kvs2", [SB_, 2048, 512])]
    x1_scr = nc.dram_tensor("x1_scr", [TH + NS, D], F32, kind="Internal").ap()
    P.tracked_dram.add("x1_scr")
    g_scr = nc.dram_tensor("g_scr", [4, 128, D], F32, kind="Internal").ap()
    P.tracked_dram.add("g_scr")

    sb = SBAlloc(nc, es, kib=207)
    ps = [es.enter_context(nc.psum_tensor(f"ps{i}", [128, 512], F32)) for i in range(8)]
    psb = [p.bitcast(BF16) for p in ps]
    bank_ctr = [0]
    reserved = set()

    def nbank():
        while True:
            b = bank_ctr[0] % 8
            bank_ctr[0] += 1
            if b not in reserved:
                return b

    def dump(name, ap, shape, dt=F32):
        t = nc.dram_tensor("dbg_" + name, list(shape), dt, kind="ExternalOutput").ap()
        dbg_outs[name] = t
        P.dma("sp", t, ap)

    w_in_v = w_in.rearrange("(k p) n -> p k n", p=128)

    def load_slab(dst, c0, n, src_v=None):
        src_v = w_in_v if src_v is None else src_v
        o = 0
        while o < n:
            m = min(1024, n - o)
            P.dma("pool", dst[:, :, o:o + m], src_v[:, :, c0 + o:c0 + o + m])
            o += m

    def load_row_bf(dst_row, src_1d, c0, n):
        P.dma("pool", dst_row, src_1d[c0:c0 + n].rearrange("(a b) -> a b", a=1))

    def cache_copy(b_):
        if dev is None or dev == "KV":
            for g, W in enumerate((128, 512, 2048)):
                P.dma("act", kvs[g][b_, 0:W - 4, :], kvc[g][b_, 4:W, :])

    cst = sb.alloc([1024], F32)
    P.dma("sp", cst, consts)
    identf = cst[:, C_ID:C_ID + 128]
    selP = cst[0:17, C_SELP:C_SELP + 128]
    selS = cst[0:17, C_SELS:C_SELS + 64]
    ones_f = cst[:, C_ONES:C_ONES + 64]
    trilf = cst[:, C_TRIL:C_TRIL + 128]
    triuf = cst[:, C_TRIU:C_TRIU + 128]
    cb = sb.alloc([512], BF16)
    identb = cb[:, 0:128]
    maskD = cb[:, 128:256]
    maskP = cb[:, 256:384]
    ones_b = cb[:, 384:512]
    P.copy(identb, identf)
    P.copy(cb[:, 128:384], cst[:, C_MD:C_MD + 256])
    P.memset(ones_b, 1.0)
    flg = sb.alloc([2], F32)
    P.dma("sp", flg, flagv)
    epsT = sb.alloc([1], F32)
    P.memset(epsT, EPS)

    P.cur_phase = "M"
    modT = sb.alloc([4, 8, 17], F32)
    A1T = sb.alloc([8, 17], F32)
    A2T = sb.alloc([8, 17], F32)
    S32 = sb.alloc([4, 256], F32)
    mk = sb.mark()
    G1p = sb.alloc([D], F32)
    G2p = sb.alloc([D], F32)
    G1s = sb.alloc([D], F32)
    G2s = sb.alloc([D], F32)
    c_sb = sb.alloc([D], F32)
    c_bf = sb.alloc([D], BF16)
    cT = sb.alloc([8, 17], BF16)
    vecs = sb.alloc([128], F32)
    vecT = sb.alloc([64], F32)
    brow = sb.alloc([2 * D], F32)
    modtok = sb.alloc([2 * D], F32)
    slabs = [sb.alloc([8, D], BF16) for _ in range(2)]

    P.dma("sp", c_sb[0:17, :], cpcs)
    P.dma("sp", vecs[0:48, :], b_ada.rearrange("(a b) -> a b", b=128))
    P.dma("sp", vecs[48:56, :], norm1_g.rearrange("(a b) -> a b", b=128))
    P.dma("sp", vecs[56:64, :], norm2_g.rearrange("(a b) -> a b", b=128))
    P.dma("sp", brow[0:1, 0:D], b_ada[2 * D:3 * D].rearrange("(a b) -> a b", a=1))
    P.dma("sp", brow[0:1, D:2 * D], b_ada[5 * D:6 * D].rearrange("(a b) -> a b", a=1))
    P.act(c_bf[0:17, :], c_sb[0:17, :], AF.Silu)
    for k in range(8):
        P.tr(psb[0][:, k * 32:k * 32 + 17], c_bf[0:17, k * 128:(k + 1) * 128], identb[0:17, 0:17])
    P.copy(cT, psb[0][:, 0:256].rearrange("p (k t) -> p k t", k=8)[:, :, 0:17])
    P.tr(ps[1][:, 0:64], vecs[0:64, :], identf[0:64, 0:64])
    P.copy(vecT, ps[1][:, 0:64])
    badaT = vecT[:, 0:48]
    n1gT = vecT[:, 48:56]
    n2gT = vecT[:, 56:64]
    w_ada_v = w_ada.rearrange("(k p) n -> p k n", p=128)
    fm_slot = {0: 0, 1: 1, 3: 2, 4: 3}
    for si_, s in enumerate((0, 1, 3, 4, 2, 5)):
        slab = slabs[si_ % 2]
        P.dma("pool", slab, w_ada_v[:, :, s * D:(s + 1) * D])
        if s in fm_slot:
            bank = ps[2 + (si_ % 2)]
            for j in range(8):
                for k in range(8):
                    P.mm(bank[:, j * 32:j * 32 + 17], slab[:, k, j * 128:(j + 1) * 128], cT[:, k, :],
                         start=(k == 0), stop=(k == 7))
            P.tt(modT[:, fm_slot[s], :, :], bank[:, 0:256].rearrange("p (j t) -> p j t", j=8)[:, :, 0:17],
                 bcast(badaT[:, 8 * s:8 * s + 8], [[1, 8], [0, 17]]), ALU.add)
            if s == 1:
                P.ts(A1T, modT[:, 1, :, :], 1.0, ALU.add)
                P.tt(A1T, A1T, bcast(n1gT, [[1, 8], [0, 17]]), ALU.mult)
            if s == 4:
                P.ts(A2T, modT[:, 3, :, :], 1.0, ALU.add)
                P.tt(A2T, A2T, bcast(n2gT, [[1, 8], [0, 17]]), ALU.mult)
        else:
            g = 0 if s == 2 else 1
            for hf in range(2):
                bank = ps[4 + hf]
                for k in range(8):
                    P.mm(bank[0:17, :], cT[:, k, :], slab[:, k, hf * 512:(hf + 1) * 512], start=(k == 0), stop=False)
                P.mm(bank[0:17, :], ones_f[0:1, 0:17], brow[0:1, g * D + hf * 512:g * D + (hf + 1) * 512],
                     start=False, stop=True)
                P.copy(modtok[0:17, g * D + hf * 512:g * D + (hf + 1) * 512], bank[0:17, :], eng="act")
    for g, (Gp, Gs) in enumerate(((G1p, G1s), (G2p, G2s))):
        for hf in range(2):
            P.mm(ps[6][:, :], selP, modtok[0:17, g * D + hf * 512:g * D + (hf + 1) * 512])
            P.copy(Gp[:, hf * 512:(hf + 1) * 512], ps[6][:, :], eng="act")
            P.mm(ps[7][0:NS, :], selS, modtok[0:17, g * D + hf * 512:g * D + (hf + 1) * 512])
            P.copy(Gs[0:NS, hf * 512:(hf + 1) * 512], ps[7][0:NS, :], eng="act")
    B1T = modT[:, 0, :, :]
    B2T = modT[:, 2, :, :]
    P.dma("sp", g_scr[0], G1p)
    P.dma("sp", g_scr[1], G2p)
    P.dma("sp", g_scr[2, 0:NS, :], G1s[0:NS, :])
    P.dma("sp", g_scr[3, 0:NS, :], G2s[0:NS, :])
    sb.release(mk)

    def mod_ap(XT, is_smp):
        if is_smp:
            return bcast(XT[:, :, 1:17], [[17, 8], [1, 16], [0, 4]])
        return bcast(XT[:, :, 0:1], [[17, 8], [0, 128]])

    def norm_T(xt, nt, is_smp, AT, BT, out_hT, bufs, i):
        nb_ = bufs.get("nb", 2)
        ssq = bufs["st"][i % nb_]
        P.act(bufs["junk"][0:nt, :], xt[0:nt, :], AF.Square, accum_out=ssq[0:nt, 0:1])
        P.act(ssq[0:nt, 1:2], ssq[0:nt, 0:1], AF.Sqrt, bias=epsT[0:nt, 0:1], scale=1.0 / D)
        P.recip(ssq[0:nt, 2:3], ssq[0:nt, 1:2])
        xn = bufs["xn"][i % nb_]
        P.ts(xn[0:nt, :], xt[0:nt, :], ssq[0:nt, 2:3], ALU.mult)
        bank = psb[nbank()]
        for k in range(8):
            P.tr(bank[:, k * 128:k * 128 + nt], xn[0:nt, k * 128:(k + 1) * 128], identb[0:nt, 0:nt])
        src = bank[:, 0:1024].rearrange("p (k t) -> p k t", k=8)[:, :, 0:nt]
        if is_smp:
            tv = bufs["tmp"][i % nb_][:, :, 0:nt]
            r4 = lambda a: a.rearrange("p k (b s) -> p k b s", s=4)
            P.tt(r4(tv), r4(src), mod_ap(AT, True), ALU.mult)
            P.tt(r4(out_hT), r4(tv), mod_ap(BT, True), ALU.add, eng="pool")
        else:
            for k in range(8):
                if k % 2 == 0 or _os.environ.get("IDEVAC", "0") != "1":
                    P.ts(out_hT[:, k, :], src[:, k, :], AT[:, k, 0:1], ALU.mult, s2=BT[:, k, 0:1], op1=ALU.add)
                else:
                    P.act(out_hT[:, k, :], src[:, k, :], AF.Identity, bias=BT[:, k, 0:1], scale=AT[:, k, 0:1])

    def norm_bufs(nb_=2):
        return dict(xt=[sb.alloc([D], F32) for _ in range(nb_)], st=[sb.alloc([4], F32) for _ in range(nb_)],
                    junk=sb.alloc([D], BF16), xn=[sb.alloc([D], BF16) for _ in range(nb_)],
                    tmp=[sb.alloc([8, NS], F32)] * nb_, nb=nb_)

    def proj_tok(out_ps, lhs_fn, nt, slab, c0, n, brow_bf):
        for k in range(8):
            P.mm(out_ps, lhs_fn(k), slab[:, k, c0:c0 + n], start=(k == 0), stop=False)
        P.mm(out_ps, ones_b[0:1, 0:nt], brow_bf[0:1, c0:c0 + n], start=False, stop=True)

    def rope(Xps, nt, nh, blk, out32, out16, tb):
        X = Xps.rearrange("p (h t d) -> p h t d", h=nh, t=2)
        t1 = tb["t1"][0:nt, 0:nh * 64].rearrange("p (h t d) -> p h t d", h=nh, t=2)
        t2 = tb["t2"][0:nt, 0:nh * 64].rearrange("p (h t d) -> p h t d", h=nh, t=2)
        rt = rt_["t"][0:nt, blk - rt_["b0"], :]
        cc = bcast(rt[:, 0:32], [[0, nh], [0, 2], [1, 32]])
        nsn = bcast(rt[:, 32:64], [[0, nh], [1, 32]])
        psn = bcast(rt[:, 64:96], [[0, nh], [1, 32]])
        P.tt(t1, X, cc, ALU.mult)
        P.tt(t2[:, :, 0, :], X[:, :, 1, :], nsn, ALU.mult)
        P.tt(t2[:, :, 1, :], X[:, :, 0, :], psn, ALU.mult)
        f = lambda a: a.rearrange("p h t d -> p (h t d)")
        if out32 is not None:
            P.tt(out32, f(t1), f(t2), ALU.add, eng="pool")
            if out16 is not None:
                P.copy(out16, out32, eng="pool")
        else:
            P.tt(out16, f(t1), f(t2), ALU.add, eng="pool")

    def gla_gate(hT_fn, nt, slab, c_glr, bglr_col, tb, triU, triL):
        bg = nbank()
        for k in range(8):
            P.mm(ps[bg][0:16, 0:nt], slab[:, k, c_glr:c_glr + 16], hT_fn(k), start=(k == 0), stop=(k == 7))
        P.ts(tb["glrT"][0:16, 0:nt], ps[bg][0:16, 0:nt], bglr_col, ALU.add)
        bz = nbank()
        P.mm(ps[bz][0:nt, :], tb["glrT"][0:16, 0:nt], wal_bf[0:16, :], start=True, stop=False)
        P.mm(ps[bz][0:nt, :], ones_b[0:1, 0:nt], bal_bf[0:1, :], start=False, stop=True)
        P.act(tb["e1"][0:nt, :], ps[bz][0:nt, :], AF.Exp, scale=-1.0)
        P.act(tb["la"][0:nt, :], tb["e1"][0:nt, :], AF.Ln, bias=1.0)
        return tb["la"]

    rt_ = {}
    wal_bf = sb.alloc([512], BF16)
    bal_bf = sb.alloc([512], BF16)
    P.dma("pool", wal_bf[0:16, :], w_alpha2)
    load_row_bf(bal_bf[0:1, :], b_alpha2, 0, 512)
    bvec = sb.alloc([128], F32)
    bcolT = sb.alloc([32], F32)
    P.memset(bvec, 0.0)
    b_in_r = lambda c0, n: b_in[c0:c0 + n].rearrange("(a b) -> a b", b=128)
    P.dma("sp", bvec[0:4, :], b_in_r(O_GQ, 512))
    P.dma("sp", bvec[4:8, :], b_in_r(O_GK, 512))
    P.dma("sp", bvec[8:16, :], b_in_r(O_GA, 1024))
    P.dma("sp", bvec[16:24, :], b_in_r(O_GB, 1024))
    P.dma("sp", bvec[24:25, 0:16], b_in[O_GLR:O_GLR + 16].rearrange("(a b) -> a b", a=1))
    P.tr(ps[0][:, 0:32], bvec[0:32, :], identf[0:32, 0:32])
    P.copy(bcolT, ps[0][:, 0:32])

    hi_mark = sb.hi
    KTp = [sb.alloc([2, 128], BF16, hi=True), sb.alloc([2, 512], BF16, hi=True), sb.alloc([2, 2048], BF16, hi=True)]
    Vfp = [sb.alloc([1, 4, 65], BF16, hi=True), sb.alloc([4, 4, 65], BF16, hi=True), sb.alloc([16, 4, 65], BF16, hi=True)]
    for g in range(3):
        P.memset(Vfp[g], 1.0)

    P.cur_phase = "P"
    mkP = sb.mark()
    SLP = 3088
    slabP = sb.alloc([8, SLP], BF16)
    rt_["t"] = sb.alloc([16, 96], F32)
    rt_["b0"] = 0
    P.dma("sp", rt_["t"], ropetab[:, 0:16, :])
    browP = sb.alloc([SLP], BF16)
    load_slab(slabP[:, :, 0:1536], O_GK, 1536)
    load_slab(slabP[:, :, 1536:3072], O_DK, 1536)
    load_slab(slabP[:, :, 3072:3088], O_GLR, 16)
    load_row_bf(browP[0:1, 0:1536], b_in, O_GK, 1536)
    load_row_bf(browP[0:1, 1536:3072], b_in, O_DK, 1536)
    NPB = 3
    nbufs = norm_bufs(NPB)
    hTp = [sb.alloc([8, 128], BF16) for _ in range(NPB)]
    tbs = [dict(t1=sb.alloc([512], F32), t2=sb.alloc([512], F32), glrT=sb.alloc([128], BF16),
                e1=sb.alloc([512], F32), la=sb.alloc([512], F32), ebs=sb.alloc([512], F32),
                kl=sb.alloc([512], BF16), vbf=sb.alloc([D], BF16), dec=sb.alloc([4], F32),
                kr16=sb.alloc([768], BF16), vn=sb.alloc([3, 4, 65], BF16)) for _ in range(NPB)]
    for t in tbs:
        P.memset(t["vn"], 1.0)
    n_prev = NBLK if dev != "P1" else 2

    def gla_state_update(la, nt, gk_ps, vbf, tb, first, triU, qmask=None):
        bs = nbank()
        P.mm(ps[bs][0:nt, :], triU, la[0:nt, :])
        P.act(tb["ebs"][0:nt, :], ps[bs][0:nt, :], AF.Exp)
        P.tt(tb["kl"][0:nt, :], gk_ps, tb["ebs"][0:nt, :], ALU.mult)

    def p_norm(j):
        xt = nbufs["xt"][j % NPB]
        P.dma("sp", xt, x_prev[j * 128:(j + 1) * 128, :])
        norm_T(xt, 128, False, A1T, B1T, hTp[j % NPB], nbufs, j)

    NA = _os.environ.get("NORM_AHEAD", "1") == "1"
    if NA:
        p_norm(0)
    for j in range(n_prev):
        tb = tbs[j % NPB]
        if NA:
            if j + 1 < n_prev:
                p_norm(j + 1)
        else:
            p_norm(j)
        h = hTp[j % NPB]
        hk = lambda k, h=h: h[:, k, :]
        bk = nbank()
        proj_tok(ps[bk][:, :], hk, 128, slabP, 0, 512, browP)
        bv = [nbank(), nbank()]
        for hf in range(2):
            proj_tok(ps[bv[hf]][:, :], hk, 128, slabP, 512 + hf * 512, 512, browP)
            P.copy(tb["vbf"][:, hf * 512:(hf + 1) * 512], ps[bv[hf]][:, :], eng="act")
        la = gla_gate(hk, 128, slabP, 3072, bcolT[0:16, 24:25], tb, triuf, trilf)
        bs = nbank()
        P.mm(ps[bs][:, :], triuf, la)
        P.act(tb["ebs"], ps[bs][:, :], AF.Exp)
        P.tt(tb["kl"], ps[bk][:, :], tb["ebs"], ALU.mult)
        bd = nbank()
        for hh in range(4):
            P.mm(ps[bd][:, hh:hh + 1], la[:, hh * 128:(hh + 1) * 128], cst[:, C_TRIL + 127:C_TRIL + 128])
        P.act(tb["dec"], ps[bd][:, 0:4], AF.Exp)
        for hp in range(2):
            bu = nbank()
            for q in range(2):
                hh = hp * 2 + q
                P.mm(ps[bu][:, q * 256:(q + 1) * 256], tb["kl"][:, hh * 128:(hh + 1) * 128],
                     tb["vbf"][:, hh * 256:(hh + 1) * 256])
            for q in range(2):
                hh = hp * 2 + q
                if j == 0:
                    P.copy(S32[:, hh, :], ps[bu][:, q * 256:(q + 1) * 256])
                else:
                    P.stt(S32[:, hh, :], S32[:, hh, :], tb["dec"][:, hh:hh + 1], ps[bu][:, q * 256:(q + 1) * 256],
                          ALU.mult, ALU.add)
        need = [g for g in range(3) if (g == 2) or (g == 1 and j >= 12) or (g == 0 and j == 15)]
        for g in need:
            dl = DSW_DIL[g]
            bkv = nbank()
            proj_tok(ps[bkv][:, 0:256], hk, 128, slabP, 1536 + g * 256, 256, browP)
            proj_tok(ps[bkv][:, 256:512], hk, 128, slabP, 2304 + g * 256, 256, browP)
            rope(ps[bkv][:, 0:256], 128, 4, j, None, tb["kr16"][:, g * 256:(g + 1) * 256], tb)
            bt = nbank()
            for c in range(2):
                P.tr(psb[bt][:, c * 128:(c + 1) * 128], tb["kr16"][:, g * 256 + c * 128:g * 256 + (c + 1) * 128], identb)
            jj = {0: 0, 1: j - 12, 2: j}[g]
            P.copy(KTp[g][:, :, jj * 128:(jj + 1) * 128], psb[bt][:, 0:256].rearrange("p (c t) -> p c t", c=2), eng="act")
            vsrc = ps[bkv][:, 256:512].rearrange("p (h d) -> p h d", h=4)
            if g == 0:
                P.copy(Vfp[0][:, 0, :, 0:64], vsrc, eng="act")
            else:
                P.copy(tb["vn"][:, g, :, 0:64], vsrc, eng="act")
                np_ = 128 // dl
                for r in range(dl):
                    P.dma("sp", Vfp[g][np_ * jj:np_ * (jj + 1), r, :, :], tb["vn"][r:128:dl, g, :, :])
    P.ts(S32, S32, flg[:, 0:1], ALU.mult)
    if dev in ("P", "P1"):
        dump("S32", S32, [128, 4, 256])
        dump("KTp2", KTp[2], [128, 2, 2048], BF16)
        dump("Vfp2", Vfp[2], [128, 16, 4, 65], BF16)
        dump("KTp0", KTp[0], [128, 2, 128], BF16)
    sb.release(mkP)

    mkPost = sb.mark()
    hT = sb.alloc([8, TH + NS], BF16)
    obT = sb.alloc([4, TH + NS], BF16)
    if dev not in ("P", "P1"):
        P.cur_phase = "N"
        mkN = sb.mark()
        nbufs = norm_bufs()
        for i in range(NBLK + 1):
            nt = 128 if i < NBLK else NS
            xt = nbufs["xt"][i % 2]
            src = x_own[i * 128:(i + 1) * 128, :] if i < NBLK else x_smp
            P.dma("sp", xt[0:nt, :], src)
            norm_T(xt, nt, i == NBLK, A1T, B1T, hT[:, :, i * 128:i * 128 + nt], nbufs, i)
        sb.release(mkN)

        P.cur_phase = "A"
        mkA = sb.mark()
        qs32 = sb.alloc([768], F32)
        ks32 = sb.alloc([768], F32)
        vs32 = sb.alloc([768], F32)
        rt_["t"] = sb.alloc([17, 96], F32)
        rt_["b0"] = 16
        P.dma("sp", rt_["t"], ropetab[:, 16:33, :])
        QsT = sb.alloc([3, 2, NS], BF16)
        KsT = sb.alloc([3, 2, NS], BF16)
        mkA2 = sb.mark()
        OTacc = sb.alloc([4, TH], F32)
        slabA = sb.alloc([8, 768], BF16)
        browA = sb.alloc([768], BF16)
        QT = sb.alloc([2, TH + NS], BF16)
        KT = sb.alloc([2, TH + NS], BF16)
        Vf = sb.alloc([16, 4, 65], BF16)
        P.memset(Vf, 1.0)
        tbs = [dict(t1=sb.alloc([512], F32), t2=sb.alloc([512], F32), q16=sb.alloc([256], BF16),
                    k32=sb.alloc([256], F32), k16=sb.alloc([256], BF16), v32=sb.alloc([256], F32),
                    vn=sb.alloc([4, 65], BF16), pexp=sb.alloc([2, 4, 128], BF16)) for _ in range(2)]
        for t in tbs:
            P.memset(t["vn"], 1.0)
        for g in range(3):
            P.cur_phase = "A"
            dl = DSW_DIL[g]
            np_ = 128 // dl
            for part in range(3):
                load_slab(slabA[:, :, part * 256:(part + 1) * 256], (O_DQ, O_DK, O_DV)[part] + g * 256, 256)
                load_row_bf(browA[0:1, part * 256:(part + 1) * 256], b_in, (O_DQ, O_DK, O_DV)[part] + g * 256, 256)
            for i in range(NBLK + 1):
                nt = 128 if i < NBLK else NS
                smp = i == NBLK
                tb = tbs[i % 2]
                t0 = i * 128
                hk = lambda k, t0=t0, nt=nt: hT[:, k, t0:t0 + nt]
                bqk = nbank()
                proj_tok(ps[bqk][0:nt, :], hk, nt, slabA, 0, 512, browA)
                bvv = nbank()
                proj_tok(ps[bvv][0:nt, 0:256], hk, nt, slabA, 512, 256, browA)
                blk = 16 + i
                if smp:
                    P.copy(vs32[0:nt, g * 256:(g + 1) * 256], ps[bvv][0:nt, 0:256], eng="act")
                    rope(ps[bqk][0:nt, 0:256], nt, 4, blk, qs32[0:nt, g * 256:(g + 1) * 256], tb["q16"][0:nt, :], tb)
                    rope(ps[bqk][0:nt, 256:512], nt, 4, blk, ks32[0:nt, g * 256:(g + 1) * 256], tb["k16"][0:nt, :], tb)
                else:
                    rope(ps[bqk][0:nt, 0:256], nt, 4, blk, None, tb["q16"][0:nt, :], tb)
                    rope(ps[bqk][0:nt, 256:512], nt, 4, blk, tb["k32"][0:nt, :], tb["k16"][0:nt, :], tb)
                bt = nbank()
                for c in range(2):
                    P.tr(psb[bt][:, c * 128:c * 128 + nt], tb["q16"][0:nt, c * 128:(c + 1) * 128], identb[0:nt, 0:nt])
                    P.tr(psb[bt][:, 256 + c * 128:256 + c * 128 + nt], tb["k16"][0:nt, c * 128:(c + 1) * 128], identb[0:nt, 0:nt])
                P.copy(QT[:, :, t0:t0 + nt], psb[bt][:, 0:256].rearrange("p (c t) -> p c t", c=2)[:, :, 0:nt], eng="act")
                P.copy(KT[:, :, t0:t0 + nt], psb[bt][:, 256:512].rearrange("p (c t) -> p c t", c=2)[:, :, 0:nt], eng="act")
                if smp:
                    P.copy(QsT[:, g, :, :], QT[:, :, t0:t0 + nt], eng="pool")
                    P.copy(KsT[:, g, :, :], KT[:, :, t0:t0 + nt], eng="pool")
                    continue
                vsrc = ps[bvv][:, 0:256].rearrange("p (h d) -> p h d", h=4)
                W = (128, 512, 2048)[g]
                if t0 >= TH - W:
                    row0 = t0 - (TH - W)
                    P.dma("sp", kvp[g][row0:row0 + 128, 0:256], tb["k32"])
                    P.copy(tb["v32"], ps[bvv][:, 0:256], eng="act")
                    P.dma("sp", kvp[g][row0:row0 + 128, 256:512], tb["v32"])
                if g == 0:
                    P.copy(Vf[:, i, :, 0:64], vsrc)
                else:
                    P.copy(tb["vn"][:, :, 0:64], vsrc)
                    for r in range(dl):
                        if g == 1:
                            P.dma("sp", Vf[32 * (i % 4):32 * (i % 4 + 1), r * 4 + i // 4, :, :], tb["vn"][r:128:dl, :, :])
                        else:
                            P.dma("sp", Vf[8 * i:8 * (i + 1), r, :, :], tb["vn"][r:128:dl, :, :])
            if dev == "Ap":
                dump("QT", QT, [128, 2, TH + NS], BF16)
                break
            P.cur_phase = "At"
            ti = 0
            for r in range(dl):
                for m in range(16 // dl):
                    tb = tbs[ti % 2]
                    ti += 1
                    q0 = dl * 128 * m + r
                    qsl = slice(q0, q0 + dl * 127 + 1, dl)
                    vidx = {0: m, 1: r * 4 + m, 2: r}[g]
                    tiles = []
                    if m >= 1:
                        p0 = dl * 128 * (m - 1) + r
                        psl = slice(p0, p0 + dl * 127 + 1, dl)
                        tiles.append((KT, psl, Vf[:, {0: m - 1, 1: r * 4 + m - 1, 2: r}[g], :, :], None))
                    else:
                        wprev = (128, 512, 2048)[g]
                        tiles.append((KTp[g], slice(r, wprev, dl), Vfp[g][:, r if g else 0, :, :], flg[:, 1:2]))
                    tiles.append((KT, qsl, Vf[:, vidx, :, :], None))
                    bsx = [nbank(), nbank()]
                    for kt, (Ksrc, ksl, Vt, bias) in enumerate(tiles):
                        for hh in range(4):
                            pp = slice((hh % 2) * 64, (hh % 2) * 64 + 64)
                            col = (kt * 2 + hh // 2) * 128
                            P.mm(ps[bsx[hh % 2]][:, col:col + 128], Ksrc[pp, hh // 2, ksl], QT[pp, hh // 2, qsl])
                    for kt, (Ksrc, ksl, Vt, bias) in enumerate(tiles):
                        for par in range(2):
                            P.act(tb["pexp"][:, kt, par::2, :],
                                  ps[bsx[par]][:, kt * 256:(kt + 1) * 256].rearrange("p (h q) -> p h q", h=2), AF.Exp,
                                  scale=DSW_SCALE, bias=bias)
                        msk = maskP if kt == 0 else maskD
                        P.tt(tb["pexp"][:, kt, :, :], tb["pexp"][:, kt, :, :], bcast(msk, [[0, 4], [1, 128]]), ALU.mult,
                             eng=("pool" if kt == 0 else "dve"))
                    bz = nbank()
                    for hh in range(4):
                        for kt, (Ksrc, ksl, Vt, bias) in enumerate(tiles):
                            P.mm(ps[bz][0:65, hh * 128:(hh + 1) * 128], Vt[:, hh, :], tb["pexp"][:, kt, hh, :],
                                 start=(kt == 0), stop=(kt == 1))
                    src = ps[bz][0:65, :].rearrange("p (h q) -> p h q", h=4)
                    dst = OTacc[0:65, :, qsl]
                    if g == 0:
                        P.copy(dst, src)
                    else:
                        P.tt(dst, src, dst, ALU.add)
            if dev == "At":
                dump("OTacc", OTacc[0:65, :, :], [65, 4, TH])
                break
        if dev not in ("Ap", "At"):
          P.recip(OTacc[64:65, :, :], OTacc[64:65, :, :])
          for hh in range(4):
            for tg in range(4):
                bb = nbank()
                P.mm(ps[bb][0:64, :], ones_f[64:65, 0:64], OTacc[64:65, hh, tg * 512:(tg + 1) * 512])
                P.tt(obT[0:64, hh, tg * 512:(tg + 1) * 512], OTacc[0:64, hh, tg * 512:(tg + 1) * 512], ps[bb][0:64, :], ALU.mult)
        if dev == "A":
            dump("obT", obT[0:64, :, 0:TH], [64, 4, TH], BF16)
            dump("QT", QT, [128, 2, TH + NS], BF16)
            dump("KT", KT, [128, 2, TH + NS], BF16)

        sb.release(mkA2)
        P.cur_phase = "A2"
        if dev not in ("Ap", "At", "A"):
            c2 = sb.alloc([272], F32)
            P.dma("sp", c2, consts2[:, 0:272])
            mask0s = c2[:, 0:4]
            nmask = sb.alloc([3, 64], BF16)
            P.copy(nmask[0:64, :, :], c2[0:64, 64:256].rearrange("p (g t) -> p g t", g=3))
            erow = sb.alloc([64, 64], BF16)
            P.dma("pool", erow.rearrange("p a b -> p (a b)")[:, 0:2048], erow_d[:, 0:2048])
            P.dma("pool", erow.rearrange("p a b -> p (a b)")[:, 2048:4096], erow_d[:, 2048:4096])
            q16s = sb.alloc([768], BF16)
            P.copy(q16s[0:NS, :], qs32[0:NS, :], eng="pool")
            vsx = sb.alloc([3, 4, 65], BF16)
            P.memset(vsx, 1.0)
            P.copy(vsx[0:NS, :, :, 0:64], vs32[0:NS, :].rearrange("p (g h d) -> p g h d", g=3, h=4), eng="pool")
            for g, W in enumerate((128, 512, 2048)):
                for kv_, srcb in enumerate((ks32, vs32)):
                    dst = bass.AP(kvs[g].tensor, (W - 4) * 512 + kv_ * 256, [[W * 512, SB_], [512, 4], [1, 256]])
                    P.dma("sp", dst, srcb[0:NS, g * 256:(g + 1) * 256])
            NCK, NQB = 7, 3
            cks = [sb.alloc([4, 512], F32) for _ in range(NCK)]
            qbs = [sb.alloc([4, 256], F32) for _ in range(NQB)]
            prod = sb.alloc([4, 4, 64], F32)
            scs = [sb.alloc([16], F32) for _ in range(2)]
            pps = [sb.alloc([16], F32) for _ in range(2)]
            tmp2 = [sb.alloc([4, 4, 65], BF16) for _ in range(2)]
            accb = nbank()
            bank_skip = accb
            first = True
            it = 0
            iters = [(b_, g) for b_ in range(SB_) for g in range(3)]
            views = {}

            def a2_front(n_):
                b_, g = iters[n_]
                ck = cks[n_ % NCK]
                qb = qbs[n_ % NQB]
                if g == 0:
                    P.dma("sp", ck[:, 0, :], kvc[0][b_])
                    Kv = bcast(ck[:, 0, 0:256], [[0, 4], [64, 4], [1, 64]])
                    Vv = bcast(ck[:, 0, 256:512], [[0, 4], [64, 4], [1, 64]])
                else:
                    src = kvc[g][b_].rearrange("(i s) e -> i s e", s=DSW_DIL[g])[:, 0:4, :]
                    P.dma("sp", ck, src)
                    Kv = ck[:, :, 0:256].rearrange("p s (h d) -> p s h d", h=4)
                    Vv = ck[:, :, 256:512].rearrange("p s (h d) -> p s h d", h=4)
                views[n_] = (Kv, Vv)
                for sp_ in range(2):
                    bq_ = nbank()
                    if bq_ == bank_skip:
                        bq_ = nbank()
                    for s2 in range(2):
                        s = sp_ * 2 + s2
                        t = 4 * b_ + s
                        P.mm(ps[bq_][:, s2 * 256:(s2 + 1) * 256], bcast(identb[0:NS, t:t + 1], [[0, 128]]),
                             q16s[0:NS, g * 256:(g + 1) * 256])
                    P.copy(qb[:, sp_ * 2:sp_ * 2 + 2, :], ps[bq_][:, :].rearrange("p (s e) -> p s e", s=2), eng="act")

            def a2_back(n_, first):
                b_, g = iters[n_]
                qb = qbs[n_ % NQB]
                sc, pp_, t2 = scs[n_ % 2], pps[n_ % 2], tmp2[n_ % 2]
                Kv, Vv = views[n_]
                P.tt(prod, Kv, qb.rearrange("p s (h d) -> p s h d", h=4), ALU.mult)
                P.op("dve", lambda e, sc=sc: e.tensor_reduce(sc, prod.rearrange("p s h d -> p (s h) d"), AX.X, ALU.add),
                     r=[prod], w=[sc])
                P.act(pp_, sc, AF.Exp, scale=DSW_SCALE)
                if g == 0:
                    P.tt(pp_.rearrange("p (s h) -> p s h", s=4), pp_.rearrange("p (s h) -> p s h", s=4),
                         bcast(mask0s, [[1, 4], [0, 4]]), ALU.mult)
                P.tt(t2[:, :, :, 0:64], Vv, bcast(pp_, [[4, 4], [1, 4], [0, 64]]), ALU.mult, eng="pool")
                P.copy(t2[:, :, :, 64], pp_.rearrange("p (s h) -> p s h", s=4), eng="pool")
                for s in range(4):
                    P.mm(ps[accb][0:NS, 0:260], erow[:, 4 * b_ + s, :], t2[:, s, :, :].rearrange("p h e -> p (h e)"),
                         start=(first and s == 0), stop=False, skip_group_check=True)

            AHEAD = 2
            for n_ in range(min(AHEAD, len(iters))):
                a2_front(n_)
            for n_ in range(len(iters)):
                if n_ + AHEAD < len(iters):
                    a2_front(n_ + AHEAD)
                a2_back(n_, n_ == 0)
            pnew = sb.alloc([12, 64], BF16)
            bsn = [nbank(), nbank()]
            bsn = [x if x != bank_skip else nbank() for x in bsn]
            for g in range(3):
                for hh in range(4):
                    pp = slice((hh % 2) * 64, (hh % 2) * 64 + 64)
                    col = (g * 2 + hh // 2) * 64
                    P.mm(ps[bsn[hh % 2]][0:NS, col:col + 64], KsT[pp, g, hh // 2, :], QsT[pp, g, hh // 2, :])
            for par in range(2):
                P.act(pnew[0:NS, :, :].rearrange("p (g h) t -> p g h t", g=3)[:, :, par::2, :],
                      ps[bsn[par]][0:NS, 0:384].rearrange("p (g h t) -> p g h t", g=3, h=2), AF.Exp, scale=DSW_SCALE)
            P.tt(pnew[0:NS, :, :].rearrange("p (g h) t -> p g h t", g=3), pnew[0:NS, :, :].rearrange("p (g h) t -> p g h t", g=3),
                 bcast(nmask[0:NS, :, :], [[64, 3], [0, 4], [1, 64]]), ALU.mult, eng="pool")
            for g in range(3):
                for hh in range(4):
                    last = (g == 2 and hh == 3)
                    P.mm(ps[accb][0:NS, hh * 65:(hh + 1) * 65], pnew[0:NS, g * 4 + hh, :], vsx[0:NS, g, hh, :],
                         start=False, stop=last, skip_group_check=True)
            accv = ps[accb][0:NS, 0:260].rearrange("p (h e) -> p h e", h=4)
            rden = sb.alloc([4], F32)
            P.recip(rden[0:NS, :], accv[:, :, 64])
            obs = sb.alloc([4, 64], BF16)
            P.tt(obs[0:NS, :, :], accv[:, :, 0:64], bcast(rden[0:NS, :], [[1, 4], [0, 64]]), ALU.mult)
            btr = nbank()
            if btr == bank_skip:
                btr = nbank()
            for hh in range(4):
                P.tr(psb[btr][0:64, hh * 64:(hh + 1) * 64], obs[0:NS, hh, :], identb[0:NS, 0:NS])
            P.copy(obT[0:64, :, TH:TH + NS], psb[btr][0:64, 0:256].rearrange("p (h t) -> p h t", h=4))
            if dev == "A2":
                dump("obTs", obT[0:64, :, TH:TH + NS], [64, 4, NS], BF16)
        sb.release(mkA)
        sb.hi = hi_mark


    P.cur_phase = "Ba"
    if dev not in ("P", "P1", "Ap", "At", "A", "A2"):
        oaT = sb.alloc([8, TH + NS], BF16)
        mkB = sb.mark()
        SLB = 3088
        slabB = sb.alloc([8, SLB], BF16)
        browB = sb.alloc([SLB], BF16)
        load_slab(slabB[:, :, 0:3072], O_GQ, 3072)
        load_slab(slabB[:, :, 3072:3088], O_GLR, 16)
        load_row_bf(browB[0:1, 512:2048], b_in, O_GK, 1536)
        load_row_bf(browB[0:1, 2048:3072], b_in, O_GR, 1024)
        S16 = sb.alloc([4, 256], BF16)
        P.copy(S16, S32, eng="act")
        gng = sb.alloc([256], F32)
        c2b = sb.alloc([80], F32)
        P.dma("sp", c2b[:, 0:64], consts2[:, 64:128])
        P.dma("sp", c2b[:, 64:80], consts2[:, 256:272])
        bcm = sb.alloc([64], BF16)
        P.copy(bcm[0:NS, :], c2b[0:NS, 0:64])
        rowmask = c2b[0:NS, 64:80]
        maskB = sb.alloc([16, 64], BF16)
        mkT = sb.mark()
        gngr = sb.alloc([256], F32)
        P.dma("sp", gngr[0:1, :], gla_norm_g.rearrange("(a b) -> a b", a=1))
        bgn = nbank()
        P.mm(ps[bgn][:, 0:256], cst[0:1, C_SELP:C_SELP + 128], gngr[0:1, :])
        P.copy(gng, ps[bgn][:, 0:256], eng="act")
        maskBf = sb.alloc([1024], F32)
        P.dma("sp", maskBf, consts2[:, 1024:2048])
        P.copy(maskB, maskBf.rearrange("p (b c) -> p b c", b=16), eng="pool")
        sb.release(mkT)
        lnsc = sb.alloc([1], F32)
        P.memset(lnsc, float(np.log(GLA_SCALE)))
        mkW = sb.mark()
        def mk_wk():
            return dict(glrT=sb.alloc([128], BF16), e1=sb.alloc([512], F32), la=sb.alloc([512], F32),
                        ebs=sb.alloc([512], F32), kl=sb.alloc([512], BF16), vbf=sb.alloc([D], BF16),
                        dec=sb.alloc([64], F32), ebT=sb.alloc([4, 128], F32), enbT=sb.alloc([4, 128], F32),
                        qgT=sb.alloc([4, 128], BF16), kdT=sb.alloc([4, 128], BF16), attb=sb.alloc([4, 128], BF16),
                        st=sb.alloc([12], F32), oa=sb.alloc([D], BF16), sg=sb.alloc([D], F32))
        wk = [mk_wk() for _ in range(2)]
        wkh = dict(wk=wk)

        def gla_front(i):
            smp = i == NBLK
            nt = NS if smp else 128
            t0 = i * 128
            wk = wkh["wk"]
            tb = wk[i % len(wk)]
            sg = tb["sg"]
            if i < NBLK and i % 2 == 0:
                cache_copy(i // 2)
            hk = lambda k: hT[:, k, t0:t0 + nt]
            triL = cst[0:NS, C_T4L:C_T4L + NS] if smp else trilf
            triU = cst[0:NS, C_T4U:C_T4U + NS] if smp else triuf
            bq, bkT = nbank(), nbank()
            for which, bnk in ((0, bq), (1, bkT)):
                for hh in range(4):
                    for k in range(8):
                        P.mm(ps[bnk][:, hh * 128:hh * 128 + nt], slabB[:, k, which * 512 + hh * 128:which * 512 + (hh + 1) * 128],
                             hk(k), start=(k == 0), stop=(k == 7))
            bk = nbank()
            proj_tok(ps[bk][0:nt, :], hk, nt, slabB, 512, 512, browB)
            if smp:
                P.memset(tb["vbf"][64:128, :], 0.0)
                P.memset(tb["attb"][64:128, :, :], 0.0)
            for hf in range(2):
                bv = nbank()
                proj_tok(ps[bv][0:nt, :], hk, nt, slabB, 1024 + hf * 512, 512, browB)
                P.copy(tb["vbf"][0:nt, hf * 512:(hf + 1) * 512], ps[bv][0:nt, :], eng=("dve" if hf == 0 else "act"))
            la = gla_gate(hk, nt, slabB, 3072, bcolT[0:16, 24:25], tb, None, None)
            bb = nbank()
            for hh in range(4):
                P.mm(ps[bb][:, hh * 128:hh * 128 + nt], la[0:nt, hh * 128:(hh + 1) * 128], triL[0:nt, 0:nt])
            bTv = ps[bb][:, :].rearrange("p (h c) -> p h c", h=4)[:, :, 0:nt]
            P.act(tb["ebT"][:, :, 0:nt], bTv, AF.Exp, bias=lnsc[:, 0:1])
            P.act(tb["enbT"][:, :, 0:nt], bTv, AF.Exp, scale=-1.0)
            if smp:
                decv = tb["dec"][:, 0:64].rearrange("p (h b) -> p h b", h=4)
                P.act(decv, ps[bb][:, :].rearrange("p (h c) -> p h c", h=4)[:, :, 3:NS:4], AF.Exp)
            else:
                P.act(tb["dec"][:, 0:4], ps[bb][:, 127:512:128], AF.Exp)
            for hh in range(4):
                P.stt(tb["qgT"][:, hh, 0:nt], ps[bq][:, hh * 128:hh * 128 + nt], bcolT[:, hh:hh + 1], tb["ebT"][:, hh, 0:nt], ALU.add, ALU.mult)
                P.stt(tb["kdT"][:, hh, 0:nt], ps[bkT][:, hh * 128:hh * 128 + nt], bcolT[:, 4 + hh:5 + hh], tb["enbT"][:, hh, 0:nt], ALU.add, ALU.mult)
            bs = nbank()
            P.mm(ps[bs][0:nt, :], triU[0:nt, 0:nt], la[0:nt, :])
            P.act(tb["ebs"][0:nt, :], ps[bs][0:nt, :], AF.Exp)
            P.tt(tb["kl"][0:nt, :], ps[bk][0:nt, :], tb["ebs"][0:nt, :], ALU.mult)
            ba = nbank()
            for hh in range(4):
                P.mm(ps[ba][0:nt, hh * 128:hh * 128 + nt], tb["kdT"][:, hh, 0:nt], tb["qgT"][:, hh, 0:nt])
            msk = bcm[0:NS, :] if smp else maskD
            P.tt(tb["attb"][0:nt, :, 0:nt], ps[ba][0:nt, :].rearrange("p (h c) -> p h c", h=4)[:, :, 0:nt],
                 bcast(msk, [[0, 4], [1, nt]]), ALU.mult)
            for hf in range(2):
                bg_ = nbank()
                proj_tok(ps[bg_][0:nt, :], hk, nt, slabB, 2048 + hf * 512, 512, browB)
                P.act(sg[0:nt, hf * 512:(hf + 1) * 512], ps[bg_][0:nt, :], AF.Silu)
            P.tt(sg[0:nt, :].rearrange("p (h v) -> p h v", h=4), sg[0:nt, :].rearrange("p (h v) -> p h v", h=4),
                 bcast(gng[0:nt, :], [[0, 4], [1, 256]]), ALU.mult, eng="pool")

        def gla_back(i):
            smp = i == NBLK
            nt = NS if smp else 128
            t0 = i * 128
            wk = wkh["wk"]
            tb = wk[i % len(wk)]
            sg = tb["sg"]
            junk = tb["oa"]
            bo = [nbank(), nbank()]
            if not smp:
                for hh in range(4):
                    o_ps = ps[bo[hh // 2]][:, (hh % 2) * 256:(hh % 2 + 1) * 256]
                    P.mm(o_ps, tb["attb"][:, hh, :], tb["vbf"][:, hh * 256:(hh + 1) * 256], start=True, stop=False)
                    P.mm(o_ps, tb["qgT"][:, hh, :], S16[:, hh, :], start=False, stop=True)
                for hp in range(2):
                    bu = nbank()
                    for q in range(2):
                        hh = hp * 2 + q
                        P.mm(ps[bu][:, q * 256:(q + 1) * 256], tb["kl"][:, hh * 128:(hh + 1) * 128], tb["vbf"][:, hh * 256:(hh + 1) * 256])
                    for q in range(2):
                        hh = hp * 2 + q
                        P.stt(S32[:, hh, :], S32[:, hh, :], tb["dec"][:, hh:hh + 1], ps[bu][:, q * 256:(q + 1) * 256], ALU.mult, ALU.add)
                P.copy(S16, S32, eng="dve")
            else:
                reserved.update(bo)
                for hh in range(4):
                    o_ps = ps[bo[hh // 2]][0:NS, (hh % 2) * 256:(hh % 2 + 1) * 256]
                    P.mm(o_ps, tb["attb"][:, hh, 0:NS], tb["vbf"][:, hh * 256:(hh + 1) * 256], start=(hh % 2 == 0), stop=False,
                         skip_group_check=True)
                NSF = len(smpb["s0f"])
                NSH = len(smpb["s0h"])

                def s_front(b_):
                    s0f = smpb["s0f"][b_ % NSF]
                    s0h = smpb["s0h"][b_ % NSH]
                    P.dma("sp", s0f, state_in[b_].rearrange("h d v -> d h v"))
                    P.copy(s0h, s0f, eng="act")
                    qgm = smpb["qgm"][b_ % NSH]
                    P.tt(qgm, tb["qgT"][:, :, 0:NS], bcast(maskB[:, b_, :], [[0, 4], [1, NS]]), ALU.mult, eng="pool")
                    klb = smpb["klb"][b_ % NSH]
                    P.ts(klb[0:NS, :], tb["kl"][0:NS, :], rowmask[:, b_:b_ + 1], ALU.mult, eng="dve")

                def s_back(b_):
                    s0f = smpb["s0f"][b_ % NSF]
                    s0h = smpb["s0h"][b_ % NSH]
                    qgm = smpb["qgm"][b_ % NSH]
                    klb = smpb["klb"][b_ % NSH]
                    for hh in range(4):
                        o_ps = ps[bo[hh // 2]][0:NS, (hh % 2) * 256:(hh % 2 + 1) * 256]
                        P.mm(o_ps, qgm[:, hh, :], s0h[:, hh, :], start=False, stop=(b_ == SB_ - 1), skip_group_check=True)
                    so = s0f
                    for hp in range(2):
                        bu = nbank()
                        for q in range(2):
                            hh = hp * 2 + q
                            P.mm(ps[bu][:, q * 256:(q + 1) * 256], klb[0:NS, hh * 128:(hh + 1) * 128], tb["vbf"][0:NS, hh * 256:(hh + 1) * 256])
                        for q in range(2):
                            hh = hp * 2 + q
                            P.stt(so[:, hh, :], s0f[:, hh, :], tb["dec"][:, hh * 16 + b_:hh * 16 + b_ + 1], ps[bu][:, q * 256:(q + 1) * 256],
                                  ALU.mult, ALU.add)
                    P.dma("sp", S_s[b_].rearrange("h d v -> d h v"), so)

                SAH = 2
                for b_ in range(min(SAH, SB_)):
                    s_front(b_)
                for b_ in range(SB_):
                    if b_ + SAH < SB_:
                        s_front(b_ + SAH)
                    s_back(b_)
            for hh in range(4):
                o_ps = ps[bo[hh // 2]][0:nt, (hh % 2) * 256:(hh % 2 + 1) * 256]
                P.act(junk[0:nt, hh * 256:(hh + 1) * 256], o_ps, AF.Square, accum_out=tb["st"][0:nt, hh:hh + 1])
            P.act(tb["st"][0:nt, 4:8], tb["st"][0:nt, 0:4], AF.Sqrt, bias=epsT[0:nt, 0:1], scale=1.0 / 256)
            P.recip(tb["st"][0:nt, 8:12], tb["st"][0:nt, 4:8])
            for hh in range(4):
                o_ps = ps[bo[hh // 2]][0:nt, (hh % 2) * 256:(hh % 2 + 1) * 256]
                P.stt(tb["oa"][0:nt, hh * 256:(hh + 1) * 256], o_ps, tb["st"][0:nt, 8 + hh:9 + hh], sg[0:nt, hh * 256:(hh + 1) * 256],
                      ALU.mult, ALU.mult)
            if smp:
                reserved.difference_update(bo)
            bt = nbank()
            for k in range(8):
                P.tr(psb[bt][:, k * 128:k * 128 + nt], tb["oa"][0:nt, k * 128:(k + 1) * 128], identb[0:nt, 0:nt])
            P.copy(oaT[:, :, t0:t0 + nt], psb[bt][:, 0:1024].rearrange("p (k t) -> p k t", k=8)[:, :, 0:nt], eng="act")

        nb_run = NBLK if dev != "B1" else 2
        gla_front(0)
        for i in range(nb_run):
            if i + 1 < nb_run:
                gla_front(i + 1)
            gla_back(i)
        if dev is None or dev in ("B", "F"):
            P.dma("sp", S_fin.rearrange("h d v -> d h v"), S32)
        sb.release(mkW)
        wkh["wk"] = [mk_wk()]
        smpb = dict(qgm=[sb.alloc([4, NS], BF16) for _ in range(3)], s0f=[sb.alloc([4, 256], F32) for _ in range(4)],
                    s0h=[sb.alloc([4, 256], BF16) for _ in range(3)], klb=[sb.alloc([512], BF16) for _ in range(3)])
        if dev != "B1":
            gla_front(NBLK)
            gla_back(NBLK)
        if dev in ("B", "B1"):
            dump("oaT", oaT, [128, 8, TH + NS], BF16)
            dump("S32", S32, [128, 4, 256])
        sb.release(mkB)

        P.cur_phase = "Bb"
        mergedT = sb.alloc([8, TH + NS], BF16, hi=True)
        mkBb = sb.mark()
        slabG = sb.alloc([8, 2048], BF16)
        wpa = sb.alloc([8, D], BF16)
        wpb = sb.alloc([4, D], BF16)
        load_slab(slabG[:, :, 0:1024], O_GA, 1024)
        load_slab(slabG[:, :, 1024:2048], O_GB, 1024)
        P.dma("pool", wpa, w_proj_a.rearrange("(k p) n -> p k n", p=128))
        P.dma("pool", wpb[0:64, :, :], w_proj_b.rearrange("(h d) n -> d h n", d=64))
        gw = [dict(sga=sb.alloc([512], BF16), sgb=sb.alloc([512], BF16), t1=sb.alloc([512], F32), t2=sb.alloc([512], F32))
              for _ in range(2)]
        it = 0
        for tg in range(5):
            t0 = tg * 512
            ntg = 512 if tg < 4 else NS
            for fc in range(8):
                w_ = gw[it % 2]
                it += 1
                bga, bpa, bgb, bpb = nbank(), nbank(), nbank(), nbank()
                for k in range(8):
                    P.mm(ps[bga][:, 0:ntg], slabG[:, k, fc * 128:(fc + 1) * 128], hT[:, k, t0:t0 + ntg], start=(k == 0), stop=(k == 7))
                P.act(w_["sga"][:, 0:ntg], ps[bga][:, 0:ntg], AF.Sigmoid, bias=bcolT[:, 8 + fc:9 + fc])
                for k in range(8):
                    P.mm(ps[bpa][:, 0:ntg], wpa[:, k, fc * 128:(fc + 1) * 128], oaT[:, k, t0:t0 + ntg], start=(k == 0), stop=(k == 7))
                P.tt(w_["t1"][:, 0:ntg], ps[bpa][:, 0:ntg], w_["sga"][:, 0:ntg], ALU.mult)
                for k in range(8):
                    P.mm(ps[bgb][:, 0:ntg], slabG[:, k, 1024 + fc * 128:1024 + (fc + 1) * 128], hT[:, k, t0:t0 + ntg], start=(k == 0), stop=(k == 7))
                P.act(w_["sgb"][:, 0:ntg], ps[bgb][:, 0:ntg], AF.Sigmoid, bias=bcolT[:, 16 + fc:17 + fc])
                for hh in range(4):
                    P.mm(ps[bpb][:, 0:ntg], wpb[0:64, hh, fc * 128:(fc + 1) * 128], obT[0:64, hh, t0:t0 + ntg], start=(hh == 0), stop=(hh == 3))
                P.tt(w_["t2"][:, 0:ntg], ps[bpb][:, 0:ntg], w_["sgb"][:, 0:ntg], ALU.mult)
                P.tt(mergedT[:, fc, t0:t0 + ntg], w_["t1"][:, 0:ntg], w_["t2"][:, 0:ntg], ALU.add, eng="pool")
        if dev == "Bb":
            dump("mergedT", mergedT, [128, 8, TH + NS], BF16)
        sb.release(mkBb)

    P.cur_phase = "B2"
    if dev is None or dev in ("B2", "F"):
        mkC = sb.mark()
        wout = sb.alloc([8, D], BF16, hi=True)
        P.dma("pool", wout, w_out.rearrange("(k p) n -> p k n", p=128))
        G1p = sb.alloc([D], F32, hi=True)
        G1s = sb.alloc([D], F32, hi=True)
        P.dma("sp", G1p, g_scr[0])
        P.dma("sp", G1s[0:NS, :], g_scr[2, 0:NS, :])
        xb = [sb.alloc([D], F32, hi=True) for _ in range(2)]
        tt_ = [sb.alloc([D], F32, hi=True) for _ in range(2)]
        for i in range(NBLK + 1):
            nt = 128 if i < NBLK else NS
            t0 = i * 128
            xt = xb[i % 2]
            P.dma("sp", xt[0:nt, :], x_own[t0:t0 + 128, :] if i < NBLK else x_smp)
            G1 = G1p if i < NBLK else G1s
            for hf in range(2):
                by = nbank()
                for k in range(8):
                    P.mm(ps[by][0:nt, :], mergedT[:, k, t0:t0 + nt], wout[:, k, hf * 512:(hf + 1) * 512], start=(k == 0), stop=(k == 7))
                P.tt(tt_[i % 2][0:nt, hf * 512:(hf + 1) * 512], ps[by][0:nt, :], G1[0:nt, hf * 512:(hf + 1) * 512], ALU.mult)
            P.tt(xt[0:nt, :], xt[0:nt, :], tt_[i % 2][0:nt, :], ALU.add, eng="dve")
            P.dma("sp", x1_scr[t0:t0 + nt, :], xt[0:nt, :])
        sb.release(mkC)
        sb.top = mkPost
        sb.hi = hi_mark

        P.cur_phase = "C"
        wup = sb.alloc([8, 2 * DFF], BF16)
        wdn = sb.alloc([22, D], BF16)
        w_up_v = w_up.rearrange("(k p) n -> p k n", p=128)
        for o_ in range(0, 2 * DFF, 1024):
            m_ = min(1024, 2 * DFF - o_)
            P.dma("pool", wup[:, :, o_:o_ + m_], w_up_v[:, :, o_:o_ + m_])
        w_dn_v = w_down.rearrange("(k p) n -> p k n", p=128)
        for k0 in range(0, 22, 8):
            k1 = min(22, k0 + 8)
            P.dma("pool", wdn[:, k0:k1, :], w_dn_v[:, k0:k1, :])
        G2p = sb.alloc([D], F32)
        G2s = S32.rearrange("p h v -> p (h v)")
        P.dma("sp", G2p, g_scr[1])
        P.dma("sp", G2s[0:NS, :], g_scr[3, 0:NS, :])
        nfg = sb.alloc([D], F32)
        mkT2 = sb.mark()
        nfr = sb.alloc([D], F32)
        P.dma("sp", nfr[0:1, :], normf_g.rearrange("(a b) -> a b", a=1))
        for hf in range(2):
            bn_ = nbank()
            P.mm(ps[bn_][:, :], cst[0:1, C_SELP:C_SELP + 128], nfr[0:1, hf * 512:(hf + 1) * 512])
            P.copy(nfg[:, hf * 512:(hf + 1) * 512], ps[bn_][:, :], eng="act")
        sb.release(mkT2)
        xn_b = [sb.alloc([D], BF16) for _ in range(2)]
        nbufs = dict(xt=[sb.alloc([D], F32) for _ in range(2)], st=[sb.alloc([4], F32) for _ in range(2)],
                     junk=None, xn=xn_b, tmp=[cst[:, 0:8 * NS].rearrange("p (k t) -> p k t", k=8)] * 2)
        xr = [sb.alloc([D], F32) for _ in range(2)]
        ttb = cst
        h2g = [sb.alloc([8, 256], BF16) for _ in range(2)]
        actTs = [sb.alloc([22, 256], BF16) for _ in range(2)]
        su = [sb.alloc([256], BF16) for _ in range(2)]
        stf = [sb.alloc([4], F32) for _ in range(2)]
        groups = [[2 * g_, 2 * g_ + 1] for g_ in range(NBLK // 2)] + [[NBLK]]
        ginfo = {}

        def c_norm(gi):
            hb = h2g[gi % 2]
            ntg = 0
            offs = []
            for i in groups[gi]:
                nt = 128 if i < NBLK else NS
                xt = nbufs["xt"][i % 2]
                P.dma("sp", xt[0:nt, :], x1_scr[i * 128:i * 128 + nt, :])
                nbufs["junk"] = xn_b[i % 2]
                norm_T(xt, nt, i == NBLK, A2T, B2T, hb[:, :, ntg:ntg + nt], nbufs, i)
                offs.append((i, ntg, nt))
                ntg += nt
            ginfo[gi] = (offs, ntg)

        if NA:
            c_norm(0)
        for gi, blks in enumerate(groups):
            hb = h2g[gi % 2]
            actT = actTs[gi % 2]
            if NA:
                if gi + 1 < len(groups):
                    c_norm(gi + 1)
            else:
                c_norm(gi)
            offs, ntg = ginfo[gi]
            if 1 <= gi <= 8:
                cache_copy(8 + gi - 1)
            for fc in range(22):
                b1, b2 = nbank(), nbank()
                for k in range(8):
                    P.mm(ps[b1][:, 0:ntg], wup[:, k, fc * 128:(fc + 1) * 128], hb[:, k, 0:ntg], start=(k == 0), stop=(k == 7))
                for k in range(8):
                    P.mm(ps[b2][:, 0:ntg], wup[:, k, DFF + fc * 128:DFF + (fc + 1) * 128], hb[:, k, 0:ntg], start=(k == 0), stop=(k == 7))
                s_ = su[fc % 2]
                P.act(s_[:, 0:ntg], ps[b1][:, 0:ntg], AF.Silu)
                P.tt(actT[:, fc, 0:ntg], ps[b2][:, 0:ntg], s_[:, 0:ntg], ALU.mult)
            for (i, o_, nt) in offs:
                smp = i == NBLK
                G2 = G2s if smp else G2p
                x1r = xr[i % 2]
                P.dma("sp", x1r[0:nt, :], x1_scr[i * 128:i * 128 + nt, :])
                for hf in range(2):
                    by = nbank()
                    for kc in range(22):
                        P.mm(ps[by][0:nt, :], actT[:, kc, o_:o_ + nt], wdn[:, kc, hf * 512:(hf + 1) * 512], start=(kc == 0), stop=(kc == 21))
                    P.tt(ttb[0:nt, hf * 512:(hf + 1) * 512], ps[by][0:nt, :], G2[0:nt, hf * 512:(hf + 1) * 512], ALU.mult)
                P.tt(x1r[0:nt, :], x1r[0:nt, :], ttb[0:nt, :], ALU.add, eng="dve")
                st = stf[i % 2]
                P.act(ttb[0:nt, :], x1r[0:nt, :], AF.Square, accum_out=st[0:nt, 0:1])
                P.act(st[0:nt, 1:2], st[0:nt, 0:1], AF.Sqrt, bias=epsT[0:nt, 0:1], scale=1.0 / D)
                P.recip(st[0:nt, 2:3], st[0:nt, 1:2])
                P.stt(x1r[0:nt, :], x1r[0:nt, :], st[0:nt, 2:3], nfg[0:nt, :], ALU.mult, ALU.mult)
                P.dma("sp", y_own[i * 128:(i + 1) * 128, :] if i < NBLK else y_smp, x1r[0:nt, :])

    P.finalize()
    es.close()
    return nc, P, dbg_outs


def make_consts():
    c = np.zeros((128, 1024), np.float32)
    c[:, C_ID:C_ID + 128] = np.eye(128, dtype=np.float32)
    c[0, C_SELP:C_SELP + 128] = 1.0
    for b in range(16):
        c[1 + b, C_SELS + 4 * b:C_SELS + 4 * b + 4] = 1.0
    c[:, C_ONES:C_ONES + 64] = 1.0
    i = np.arange(128)
    c[:, C_TRIL:C_TRIL + 128] = np.where(i[:, None] <= i[None, :], -1.0 / 16, 0.0)
    c[:, C_TRIU:C_TRIU + 128] = np.where(i[:, None] > i[None, :], -1.0 / 16, 0.0)
    c[:, C_MD:C_MD + 128] = (i[:, None] <= i[None, :]).astype(np.float32)
    c[:, C_MP:C_MP + 128] = (i[:, None] >= i[None, :]).astype(np.float32)
    j = np.arange(64)
    same = (j[:, None] // 4) == (j[None, :] // 4)
    c[0:64, C_T4L:C_T4L + 64] = np.where(same & (j[:, None] <= j[None, :]), -1.0 / 16, 0.0)
    c[0:64, C_T4U:C_T4U + 64] = np.where(same & (j[:, None] > j[None, :]), -1.0 / 16, 0.0)
    return c


def make_consts2():
    c = np.zeros((128, 2048), np.float32)
    i = np.arange(128)
    c[:, 0:4] = (i[:, None] >= np.arange(4)[None, :]).astype(np.float32)
    j = np.arange(64)
    same = (j[:, None] // 4) == (j[None, :] // 4)
    c[0:64, 64:128] = (same & (j[:, None] <= j[None, :])).astype(np.float32)
    c[0:64, 128:192] = np.eye(64, dtype=np.float32)
    c[0:64, 192:256] = np.eye(64, dtype=np.float32)
    c[0:64, 256:272] = (j[:, None] // 4 == np.arange(16)[None, :]).astype(np.float32)
    mb = (np.arange(64)[None, :] // 4 == np.arange(16)[:, None]).astype(np.float32)
    c[:, 1024:2048] = mb.reshape(1, 1024)
    return c


def make_erow():
    e = np.zeros((128, 64, 64), np.float32)
    for t in range(64):
        e[:, t, t] = 1.0
    return e.reshape(128, 4096)


def make_ropetab(half):
    inv = (np.float32(10000.0) ** (-(np.arange(32, dtype=np.float32)) / np.float32(32))).astype(np.float32)
    tab = np.zeros((128, 33, 96), np.float32)
    p = np.arange(128)
    for blk in range(33):
        if blk < 16:
            pos = (half - 1) * TH + blk * 128 + p
        elif blk < 32:
            pos = half * TH + (blk - 16) * 128 + p
        else:
            pos = 8192 + (p % 4)
        ang = (pos.astype(np.float32)[:, None] * inv[None, :]).astype(np.float32)
        cs, sn = np.cos(ang).astype(np.float32), np.sin(ang).astype(np.float32)
        tab[:, blk, 0:32] = cs
        tab[:, blk, 32:64] = -sn
        tab[:, blk, 64:96] = sn
    return tab


def make_in_maps(inputs, cores, sample_base=None):
    consts = make_consts()
    consts2 = make_consts2()
    erow = make_erow()
    maps = []
    for ci, c in enumerate(cores):
        b, half = c // 2, c % 2
        xp = inputs["x_prompt"]
        s0 = SB_ * c
        flagv = np.zeros((128, 2), np.float32)
        flagv[:, 0] = float(half)
        flagv[:, 1] = (float(half) - 1.0) * 30000.0
        m = {
            "x_own": np.ascontiguousarray(xp[b, half * TH:(half + 1) * TH]),
            "x_prev": np.ascontiguousarray(xp[b, 0:TH]) if half == 1 else np.zeros((TH, D), np.float32),
            "x_smp": np.ascontiguousarray(inputs["x_sample"][s0:s0 + SB_].reshape(NS, D)),
            "cpcs": np.ascontiguousarray(np.concatenate([inputs["c_prompt"][b:b + 1], inputs["c_sample"][s0:s0 + SB_]], 0)),
            "consts": consts, "consts2": consts2, "ropetab": make_ropetab(half), "flagv": flagv, "erow_d": erow,
            "state_in": np.ascontiguousarray(inputs["state_gla"][0, s0:s0 + SB_]),
            "kvc0": np.ascontiguousarray(inputs["cache_kv_w128"][0, s0:s0 + SB_].reshape(SB_, 128, 512)),
            "kvc1": np.ascontiguousarray(inputs["cache_kv_w512"][0, s0:s0 + SB_].reshape(SB_, 512, 512)),
            "kvc2": np.ascontiguousarray(inputs["cache_kv_w2048"][0, s0:s0 + SB_].reshape(SB_, 2048, 512)),
            "normf_g": np.ascontiguousarray(inputs["normf_g"]),
        }
        for k in ("w_ada", "b_ada", "norm1_g", "norm2_g", "w_in", "b_in", "w_alpha2", "b_alpha2", "gla_norm_g",
                  "w_proj_a", "w_proj_b", "w_out", "w_up", "w_down"):
            m[k] = np.ascontiguousarray(inputs[k][0])
        maps.append(m)
    return maps


_CACHE = {}


def kernel(**inputs):
    inputs = {k: np.asarray(v) for k, v in inputs.items()}
    if "nc" not in _CACHE:
        _CACHE["nc"] = build()[0]
    nc = _CACHE["nc"]
    cores = list(range(NCORES))
    maps = make_in_maps(inputs, cores)
    res = run_bass_kernel_spmd(nc, maps, core_ids=cores).results
    return assemble(res, cores)


def assemble(res, cores):
    nb = max(c // 2 for c in cores) + 1
    nsb = SB_ * len(cores)
    y_p = np.zeros((nb, 2 * TH, D), np.float32)
    y_s = np.zeros((nsb, 4, D), np.float32)
    S_p = np.zeros((1, nb, 4, 128, 256), np.float32)
    kvp = [np.zeros((1, nb, W, 2, 4, 64), np.float32) for W in (128, 512, 2048)]
    S_s = np.zeros((1, nsb, 4, 128, 256), np.float32)
    kvs = [np.zeros((1, nsb, W, 2, 4, 64), np.float32) for W in (128, 512, 2048)]
    for ci, c in enumerate(cores):
        r = res[ci]
        b, half = c // 2, c % 2
        y_p[b, half * TH:(half + 1) * TH] = r["y_own"]
        y_s[SB_ * ci:SB_ * (ci + 1)] = np.asarray(r["y_smp"]).reshape(SB_, 4, D)
        S_s[0, SB_ * ci:SB_ * (ci + 1)] = r["S_s"]
        for g, W in enumerate((128, 512, 2048)):
            kvs[g][0, SB_ * ci:SB_ * (ci + 1)] = np.asarray(r[f"kvs{g}"]).reshape(SB_, W, 2, 4, 64)
        if half == 1:
            S_p[0, b] = r["S_fin"]
            for g, W in enumerate((128, 512, 2048)):
                kvp[g][0, b] = np.asarray(r[f"kvp{g}"]).reshape(W, 2, 4, 64)
    return (y_p, y_s, S_p, kvp[0], kvp[1], kvp[2], S_s, kvs[0], kvs[1], kvs[2])
```
